# Optimizing a Trainium2 kernel written in Bass

```python
import math, functools
import jax, jax.numpy as jnp
from jax import lax
import numpy as np

D_MODEL = 1024
BATCH = 32
SEQ = 256
DEPTH = 2
DEC_BATCH = 2
DEC_SEQ = 2048
PAST_LEN = 256

GRID_W = 64
HEAD_DIM = 64
NA_HEADS = 4
NA_WIN_R = 8
NA_WIN_C = 16
MLA_HEADS = 4
MLA_NOPE = 64
MLA_ROPE = 32
MLA_V = 64
MLA_Q_RANK = 192
MLA_KV_RANK = 128
SWA_HEADS = 4
SWA_KV_HEADS = 2
SWA_WINDOW = 128
SWA_BLOCK = 128
POOL_WINDOWS = (2, 4, 8, 16)
POOL_GROUP = 64
N_POOL = len(POOL_WINDOWS)
POOL_WIDTH = N_POOL * POOL_GROUP
BRANCH_W = 256
N_BRANCH = 4
D_FF = 2752
CONV_W = 3
Q_BLOCK = 128
ROPE_BASE = 10000.0
EPS = 1e-6
NEG_INF = -1e30
ATT_SCALE = HEAD_DIM ** -0.5
MLA_SCALE = (MLA_NOPE + MLA_ROPE) ** -0.5

IN_A = 3 * NA_HEADS * HEAD_DIM
IN_B = MLA_Q_RANK + MLA_KV_RANK + MLA_ROPE
IN_C = (SWA_HEADS + 2 * SWA_KV_HEADS) * HEAD_DIM
IN_D = POOL_WIDTH
IN_DIM = IN_A + IN_B + IN_C + IN_D

kernel_name = 'hybrid_diffusion_prefix_trunk_step'


def rmsnorm(x, g):
    xf = x.astype(jnp.float32)
    y = xf * lax.rsqrt(jnp.mean(xf * xf, -1, keepdims=True) + EPS)
    return (y * g.astype(jnp.float32)).astype(x.dtype)


def _heads(t, n):
    return t.reshape(*t.shape[:-1], n, t.shape[-1] // n)


def _rope_1d(x, pos):
    half = x.shape[-1] // 2
    inv = ROPE_BASE ** (-jnp.arange(half, dtype=jnp.float32) / half)
    ang = pos.astype(jnp.float32)[:, None] * inv[None, :]
    cos = jnp.cos(ang)[:, None, :]
    sin = jnp.sin(ang)[:, None, :]
    xf = x.astype(jnp.float32)
    x1, x2 = xf[..., :half], xf[..., half:]
    return jnp.concatenate([x1 * cos - x2 * sin, x1 * sin + x2 * cos], -1).astype(x.dtype)


def rope_2d(x):
    t = jnp.arange(x.shape[1])
    h = x.shape[-1] // 2
    return jnp.concatenate([_rope_1d(x[..., :h], t // GRID_W), _rope_1d(x[..., h:], t % GRID_W)], -1)


def sink_softmax(s, sink):
    m = jnp.maximum(jnp.max(s, -1, keepdims=True), sink)
    e = jnp.exp(s - m)
    return e / (jnp.sum(e, -1, keepdims=True) + jnp.exp(sink - m))


def sweep_queries(fn, *qs):
    blocks = tuple(jnp.moveaxis(t.reshape(t.shape[0], t.shape[1] // Q_BLOCK, Q_BLOCK, *t.shape[2:]), 1, 0)
                   for t in qs)
    out = jnp.moveaxis(lax.map(lambda bl: fn(*bl), blocks), 0, 1)
    return out.reshape(out.shape[0], out.shape[1] * out.shape[2], *out.shape[3:])


def softmax_attend(q, k, v):
    s = jnp.einsum('bqhd,bkhd->bhqk', q, k).astype(jnp.float32) * ATT_SCALE
    p = jax.nn.softmax(s, -1).astype(v.dtype)
    return jnp.einsum('bhqk,bkhd->bqhd', p, v)


def gqa_sink_attend(q, k, v, sink):
    B, Lq, H, d = q.shape
    kvh = k.shape[2]
    g = H // kvh
    qg = q.reshape(B, Lq, kvh, g, d)
    s = jnp.einsum('bqhgd,bkhd->bhgqk', qg, k).astype(jnp.float32) * ATT_SCALE
    p = sink_softmax(s, sink.reshape(1, kvh, g, 1, 1).astype(jnp.float32)).astype(v.dtype)
    return jnp.einsum('bhgqk,bkhd->bqhgd', p, v).reshape(B, Lq, H, d)


def mla_queries(c_q, q_norm, w_uq):
    q = _heads(rmsnorm(c_q, q_norm) @ w_uq, MLA_HEADS)
    return q[..., :MLA_NOPE], q[..., MLA_NOPE:]


def mla_keys_values(c_kv, kv_norm, w_ukv):
    kv = _heads(rmsnorm(c_kv, kv_norm) @ w_ukv, MLA_HEADS)
    return kv[..., :MLA_NOPE], kv[..., MLA_NOPE:]


def mla_attend(qn, qr, kn, kr, v):
    s = (jnp.einsum('bqhd,bkhd->bhqk', qn, kn) + jnp.einsum('bqhr,bkr->bhqk', qr, kr)).astype(jnp.float32) * MLA_SCALE
    p = jax.nn.softmax(s, -1).astype(v.dtype)
    return jnp.einsum('bhqk,bkhd->bqhd', p, v)


def na_latent(q, k, v, k_ctx, v_ctx, rpb):
    B, L, H, d = q.shape
    rows = L // GRID_W
    kr = min(NA_WIN_R, rows)
    qg = q.reshape(B, rows, GRID_W, H, d)
    kg = k.reshape(B, rows, GRID_W, H, d)
    vg = v.reshape(B, rows, GRID_W, H, d)
    r = jnp.arange(rows)
    r_idx = jnp.clip(r - kr // 2, 0, rows - kr)[:, None] + jnp.arange(kr)[None, :]
    cq = jnp.arange(GRID_W)
    c_start = jnp.clip(cq - NA_WIN_C // 2, 0, GRID_W - NA_WIN_C)
    col_ok = (cq[None, :] >= c_start[:, None]) & (cq[None, :] < c_start[:, None] + NA_WIN_C)
    k_rows = kg[:, r_idx]
    v_rows = vg[:, r_idx]
    s = jnp.einsum('brchd,brjwhd->bhrcjw', qg, k_rows).astype(jnp.float32) * ATT_SCALE
    dr = r_idx - r[:, None] + (NA_WIN_R - 1)
    dc = jnp.clip(cq[None, :] - cq[:, None] + (NA_WIN_C - 1), 0, 2 * NA_WIN_C - 2)
    bias = rpb[:, dr[:, None, :, None], dc[None, :, None, :]].astype(jnp.float32)
    s = jnp.where(col_ok[:, None, :], s + bias[None], NEG_INF)
    s = s.reshape(B, H, rows, GRID_W, kr * GRID_W)
    s_ctx = jnp.einsum('brchd,bkhd->bhrck', qg, k_ctx).astype(jnp.float32) * ATT_SCALE
    p = jax.nn.softmax(jnp.concatenate([s, s_ctx], -1), -1).astype(v.dtype)
    p_lat = p[..., :kr * GRID_W].reshape(B, H, rows, GRID_W, kr, GRID_W)
    y = (jnp.einsum('bhrcjw,brjwhd->brchd', p_lat, v_rows)
         + jnp.einsum('bhrck,bkhd->brchd', p[..., kr * GRID_W:], v_ctx))
    return y.reshape(B, L, H * d)


def swa_latent(q, k, v, k_ctx, v_ctx, sink):
    B, L, H, d = q.shape
    kvh = k.shape[2]
    g = H // kvh
    nb = L // SWA_BLOCK
    side = -(-SWA_WINDOW // SWA_BLOCK)
    span = (2 * side + 1) * SWA_BLOCK
    pad = side * SWA_BLOCK
    k_pad = jnp.pad(k, ((0, 0), (pad, pad), (0, 0), (0, 0)))
    v_pad = jnp.pad(v, ((0, 0), (pad, pad), (0, 0), (0, 0)))
    idx = jnp.arange(nb)[:, None] * SWA_BLOCK + jnp.arange(span)[None, :]
    kb = k_pad[:, idx]
    vb = v_pad[:, idx]
    qb = q.reshape(B, nb, SWA_BLOCK, kvh, g, d)
    s = jnp.einsum('bnqhgd,bnjhd->bhgnqj', qb, kb).astype(jnp.float32) * ATT_SCALE
    q_pos = jnp.arange(L).reshape(nb, SWA_BLOCK)
    k_pos = (idx - pad)[:, None, :]
    valid = (jnp.abs(q_pos[:, :, None] - k_pos) <= SWA_WINDOW) & (k_pos >= 0) & (k_pos < L)
    s = jnp.where(valid, s, NEG_INF)
    s_ctx = jnp.einsum('bnqhgd,bkhd->bhgnqk', qb, k_ctx).astype(jnp.float32) * ATT_SCALE
    p = sink_softmax(jnp.concatenate([s, s_ctx], -1),
                     sink.reshape(1, kvh, g, 1, 1, 1).astype(jnp.float32)).astype(v.dtype)
    y = (jnp.einsum('bhgnqj,bnjhd->bnqhgd', p[..., :span], vb)
         + jnp.einsum('bhgnqk,bkhd->bnqhgd', p[..., span:], v_ctx))
    return y.reshape(B, L, H * d)


def swa_split(pc):
    q, k, v = jnp.split(pc, [SWA_HEADS * HEAD_DIM, (SWA_HEADS + SWA_KV_HEADS) * HEAD_DIM], -1)
    return _heads(q, SWA_HEADS), _heads(k, SWA_KV_HEADS), _heads(v, SWA_KV_HEADS)


def pool_mixer(x, pool_w, pool_scale):
    B, L, _ = x.shape
    xf = x.astype(jnp.float32)
    cs = jnp.concatenate([jnp.zeros((B, 1, POOL_WIDTH), jnp.float32), jnp.cumsum(xf, axis=1)], 1)
    t = jnp.arange(L)
    diffs = []
    for gi, w in enumerate(POOL_WINDOWS):
        lo = jnp.clip(t - w // 2, 0, L)
        hi = jnp.clip(t - w // 2 + w, 0, L)
        sl = slice(gi * POOL_GROUP, (gi + 1) * POOL_GROUP)
        mean = (cs[:, hi, sl] - cs[:, lo, sl]) / (hi - lo).astype(jnp.float32)[:, None]
        diffs.append(mean - xf[:, :, sl])
    dlt = jnp.stack(diffs, 2).astype(x.dtype)
    y = jnp.einsum('blgc,gce->blge', dlt, pool_w).reshape(B, L, POOL_WIDTH)
    return y * pool_scale


def conv_ffn(h, w_up, conv_w, w_down):
    L = h.shape[1]
    u = h @ w_up
    pad = CONV_W // 2
    up = jnp.pad(u, ((0, 0), (pad, CONV_W - 1 - pad), (0, 0)))
    uc = up[:, 0:L] * conv_w[0]
    for j in range(1, CONV_W):
        uc = uc + up[:, j:j + L] * conv_w[j]
    a, gt = jnp.split(uc, 2, -1)
    return (jax.nn.silu(gt) * a) @ w_down


def context_mixers(pa, pb, pc, pd, lp):
    B, L, _ = pa.shape
    qa, ka, va = (_heads(t, NA_HEADS) for t in jnp.split(pa, 3, -1))
    ya = sweep_queries(lambda q: softmax_attend(q, ka, va), qa).reshape(B, L, BRANCH_W)
    c_q, c_kv, k_rope = jnp.split(pb, [MLA_Q_RANK, MLA_Q_RANK + MLA_KV_RANK], -1)
    qn, qr = mla_queries(c_q, lp['mla_q_norm'], lp['mla_w_uq'])
    kn, vb = mla_keys_values(c_kv, lp['mla_kv_norm'], lp['mla_w_ukv'])
    yb = sweep_queries(lambda a, b_: mla_attend(a, b_, kn, k_rope, vb), qn, qr).reshape(B, L, BRANCH_W)
    qc, kc, vc = swa_split(pc)
    yc = sweep_queries(lambda q: gqa_sink_attend(q, kc, vc, lp['swa_sink']), qc).reshape(B, L, BRANCH_W)
    yd = pool_mixer(pd, lp['pool_w'], lp['pool_scale'])
    return (ya, yb, yc, yd), (ka, va, c_kv, k_rope, kc, vc)


def latent_mixers(pa, pb, pc, pd, lp, cache):
    na_k, na_v, mla_ckv, mla_krope, swa_k, swa_v = cache
    B, L, _ = pa.shape
    qa, ka, va = (_heads(t, NA_HEADS) for t in jnp.split(pa, 3, -1))
    ya = na_latent(qa, ka, va, na_k, na_v, lp['na_rpb'])
    c_q, c_kv, k_rope = jnp.split(pb, [MLA_Q_RANK, MLA_Q_RANK + MLA_KV_RANK], -1)
    qn, qr = mla_queries(c_q, lp['mla_q_norm'], lp['mla_w_uq'])
    qr = rope_2d(qr)
    kn, vb = mla_keys_values(c_kv, lp['mla_kv_norm'], lp['mla_w_ukv'])
    kr = rope_2d(k_rope[:, :, None, :])[:, :, 0]
    kn_ctx, v_ctx = mla_keys_values(mla_ckv, lp['mla_kv_norm'], lp['mla_w_ukv'])
    kn_all = jnp.concatenate([kn_ctx, kn], 1)
    kr_all = jnp.concatenate([mla_krope, kr], 1)
    v_all = jnp.concatenate([v_ctx, vb], 1)
    yb = sweep_queries(lambda a, b_: mla_attend(a, b_, kn_all, kr_all, v_all), qn, qr).reshape(B, L, BRANCH_W)
    qc, kc, vc = swa_split(pc)
    yc = swa_latent(rope_2d(qc), rope_2d(kc), vc, swa_k, swa_v, lp['swa_sink'])
    yd = pool_mixer(pd, lp['pool_w'], lp['pool_scale'])
    return (ya, yb, yc, yd), ()


def sandwich_layer(x, cvec, mixers, lp):
    mod = jax.nn.silu(cvec) @ lp['w_mod'] + lp['b_mod']
    sh1, sc1, g1, sh2, sc2, g2 = jnp.split(mod, 6, -1)
    h = rmsnorm(x, lp['g_attn_pre']) * (1 + sc1) + sh1
    pa, pb, pc, pd = jnp.split(h @ lp['w_in'], [IN_A, IN_A + IN_B, IN_A + IN_B + IN_C], -1)
    branches, ctx_state = mixers(pa, pb, pc, pd)
    br = jnp.stack(branches, 2)
    gates = jax.nn.sigmoid(h @ lp['w_gate'] + lp['b_gate']).reshape(h.shape[0], h.shape[1], N_BRANCH, D_MODEL)
    merged = jnp.sum(gates * jnp.einsum('blkc,kcd->blkd', br, lp['w_branch']), 2)
    x = x + g1 * rmsnorm(merged @ lp['w_out'], lp['g_attn_post'])
    h = rmsnorm(x, lp['g_ffn_pre']) * (1 + sc2) + sh2
    x = x + g2 * rmsnorm(conv_ffn(h, lp['ffn_w_up'], lp['ffn_conv'], lp['ffn_w_down']), lp['g_ffn_post'])
    return x, ctx_state


def setup_inputs(seed: int = 0) -> dict:
    key = jax.random.key(seed)
    ks = jax.random.split(key, 34)
    f32 = jnp.float32

    def nrm(i, shape, scale=1.0):
        return jax.random.normal(ks[i], shape, f32) * scale

    D = D_MODEL
    return {
        'x_prompt': nrm(0, (BATCH, SEQ, D)),
        'x_sample': nrm(1, (DEC_BATCH, DEC_SEQ, D)),
        'cache_na_k': nrm(2, (DEC_BATCH, DEPTH, PAST_LEN, NA_HEADS, HEAD_DIM)),
        'cache_na_v': nrm(3, (DEC_BATCH, DEPTH, PAST_LEN, NA_HEADS, HEAD_DIM)),
        'cache_mla_ckv': nrm(4, (DEC_BATCH, DEPTH, PAST_LEN, MLA_KV_RANK)),
        'cache_mla_krope': nrm(5, (DEC_BATCH, DEPTH, PAST_LEN, MLA_ROPE)),
        'cache_swa_k': nrm(6, (DEC_BATCH, DEPTH, PAST_LEN, SWA_KV_HEADS, HEAD_DIM)),
        'cache_swa_v': nrm(7, (DEC_BATCH, DEPTH, PAST_LEN, SWA_KV_HEADS, HEAD_DIM)),
        'c': nrm(8, (DEC_BATCH, D)),
        'c_ctx': nrm(9, (D,)),
        'w_mod': nrm(10, (DEPTH, D, 6 * D), 0.3 * D ** -0.5),
        'b_mod': nrm(11, (DEPTH, 6 * D), 0.02),
        'g_attn_pre': 1.0 + nrm(12, (DEPTH, D), 0.05),
        'g_attn_post': 1.0 + nrm(13, (DEPTH, D), 0.05),
        'g_ffn_pre': 1.0 + nrm(14, (DEPTH, D), 0.05),
        'g_ffn_post': 1.0 + nrm(15, (DEPTH, D), 0.05),
        'w_in': nrm(16, (DEPTH, D, IN_DIM), D ** -0.5),
        'w_gate': nrm(17, (DEPTH, D, N_BRANCH * D), D ** -0.5),
        'b_gate': nrm(18, (DEPTH, N_BRANCH * D), 0.02),
        'na_rpb': nrm(19, (DEPTH, NA_HEADS, 2 * NA_WIN_R - 1, 2 * NA_WIN_C - 1), 0.1),
        'mla_q_norm': 1.0 + nrm(20, (DEPTH, MLA_Q_RANK), 0.05),
        'mla_w_uq': nrm(21, (DEPTH, MLA_Q_RANK, MLA_HEADS * (MLA_NOPE + MLA_ROPE)), MLA_Q_RANK ** -0.5),
        'mla_kv_norm': 1.0 + nrm(22, (DEPTH, MLA_KV_RANK), 0.05),
        'mla_w_ukv': nrm(23, (DEPTH, MLA_KV_RANK, MLA_HEADS * (MLA_NOPE + MLA_V)), MLA_KV_RANK ** -0.5),
        'swa_sink': nrm(24, (DEPTH, SWA_HEADS), 0.5),
        'pool_w': nrm(25, (DEPTH, N_POOL, POOL_GROUP, POOL_GROUP), POOL_GROUP ** -0.5),
        'pool_scale': 1.0 + nrm(26, (DEPTH, POOL_WIDTH), 0.1),
        'w_branch': nrm(27, (DEPTH, N_BRANCH, BRANCH_W, D), BRANCH_W ** -0.5),
        'w_out': nrm(28, (DEPTH, D, D), D ** -0.5),
        'ffn_w_up': nrm(29, (DEPTH, D, 2 * D_FF), D ** -0.5),
        'ffn_conv': nrm(30, (DEPTH, CONV_W, 2 * D_FF), CONV_W ** -0.5),
        'ffn_w_down': nrm(31, (DEPTH, D_FF, D), D_FF ** -0.5),
    }


def reference(x_prompt, x_sample, cache_na_k, cache_na_v, cache_mla_ckv, cache_mla_krope, cache_swa_k,
              cache_swa_v, c, c_ctx, w_mod, b_mod, g_attn_pre, g_attn_post, g_ffn_pre, g_ffn_post, w_in,
              w_gate, b_gate, na_rpb, mla_q_norm, mla_w_uq, mla_kv_norm, mla_w_ukv, swa_sink, pool_w,
              pool_scale, w_branch, w_out, ffn_w_up, ffn_conv, ffn_w_down):
    x_p, x_s = x_prompt, x_sample
    c_ctx_vec = c_ctx[None, None, :]
    c_lat = c[:, None, :]
    states = []
    for i in range(DEPTH):
        lp = dict(w_mod=w_mod[i], b_mod=b_mod[i], g_attn_pre=g_attn_pre[i], g_attn_post=g_attn_post[i],
                  g_ffn_pre=g_ffn_pre[i], g_ffn_post=g_ffn_post[i], w_in=w_in[i], w_gate=w_gate[i],
                  b_gate=b_gate[i], na_rpb=na_rpb[i], mla_q_norm=mla_q_norm[i], mla_w_uq=mla_w_uq[i],
                  mla_kv_norm=mla_kv_norm[i], mla_w_ukv=mla_w_ukv[i], swa_sink=swa_sink[i],
                  pool_w=pool_w[i], pool_scale=pool_scale[i], w_branch=w_branch[i], w_out=w_out[i],
                  ffn_w_up=ffn_w_up[i], ffn_conv=ffn_conv[i], ffn_w_down=ffn_w_down[i])
        x_p, st = sandwich_layer(x_p, c_ctx_vec, functools.partial(context_mixers, lp=lp), lp)
        states.append(st)
        cache = (cache_na_k[:, i], cache_na_v[:, i], cache_mla_ckv[:, i], cache_mla_krope[:, i],
                 cache_swa_k[:, i], cache_swa_v[:, i])
        x_s, _ = sandwich_layer(x_s, c_lat, functools.partial(latent_mixers, lp=lp, cache=cache), lp)
    new_na_k = jnp.stack([st[0] for st in states], 1)
    new_na_v = jnp.stack([st[1] for st in states], 1)
    new_mla_ckv = jnp.stack([st[2] for st in states], 1)
    new_mla_krope = jnp.stack([st[3] for st in states], 1)
    new_swa_k = jnp.stack([st[4] for st in states], 1)
    new_swa_v = jnp.stack([st[5] for st in states], 1)
    return (x_p, x_s, new_na_k, new_na_v, new_mla_ckv, new_mla_krope, new_swa_k, new_swa_v)
```

```python
import contextlib
import numpy as np
import concourse.bass as bass
import concourse.mybir as mybir
from concourse.bass_utils import run_bass_kernel_spmd

F32 = mybir.dt.float32
BF16 = mybir.dt.bfloat16
AF = mybir.ActivationFunctionType
ALU = mybir.AluOpType

ENGS = ("pe", "act", "dve", "pool", "sp")
SAME_ENG_SYNC = {"pe": False, "act": True, "dve": True, "pool": True, "sp": False}


class Buf:
    __slots__ = ("name", "lw", "rd", "dsem")

    def __init__(self, name):
        self.name = name
        self.lw = []
        self.rd = []
        self.dsem = None


class DmaSem:
    __slots__ = ("name", "issued", "handle")

    def __init__(self, name):
        self.name = name
        self.issued = 0
        self.handle = None


class Tok:
    __slots__ = ("kind", "a", "b")

    def __init__(self, kind, a, b):
        self.kind, self.a, self.b = kind, a, b


class Op:
    __slots__ = ("eng", "fn", "waits", "signal", "idx", "dsem", "is_dma", "inc")

    def __init__(self, eng, fn, idx, is_dma=False, dsem=None):
        self.eng, self.fn, self.idx = eng, fn, idx
        self.inc = 16
        self.waits = []
        self.signal = False
        self.is_dma = is_dma
        self.dsem = dsem


class Sched:
    def __init__(self, nc):
        self.nc = nc
        self.ops = {e: [] for e in ENGS}
        self.dsems = []
        self.final_tokens = []

    def _deps(self, op, reads, writes):
        need = []
        for b in reads:
            need.extend(b.lw)
        for b in writes:
            need.extend(b.lw)
            need.extend(b.rd)
        best = {}
        for t in need:
            if t.kind == "d":
                k_ = id(t.a)
                if k_ not in best or best[k_].b < t.b:
                    best[k_] = t
        for t in need:
            if t.kind == "e":
                if t.a == op.eng and not SAME_ENG_SYNC[op.eng]:
                    continue
                self.ops[t.a][t.b].signal = True
            elif best[id(t.a)] is not t:
                continue
            op.waits.append(t)

    def op(self, eng, fn, reads=(), writes=()):
        lst = self.ops[eng]
        o = Op(eng, fn, len(lst))
        self._deps(o, reads, writes)
        lst.append(o)
        tok = Tok("e", eng, o.idx)
        for b in reads:
            b.rd.append(tok)
        for b in writes:
            b.lw = [tok]
            b.rd = []
        return o

    def dma(self, queue, out_ap, in_ap, reads=(), writes=(), sem_buf=None, partial=False, final=False):
        if sem_buf is None:
            sem_buf = writes[0] if writes else reads[0]
        if sem_buf.dsem is None:
            sem_buf.dsem = DmaSem(sem_buf.name)
            self.dsems.append(sem_buf.dsem)
        ds = sem_buf.dsem
        lst = self.ops[queue]

        def fn(e, out_ap=out_ap, in_ap=in_ap):
            return e.dma_start(out=out_ap, in_=in_ap)

        o = Op(queue, fn, len(lst), is_dma=True, dsem=ds)
        if partial:
            saved = [(b, b.lw) for b in writes]
            for b in writes:
                b.lw = [t for t in b.lw if not (t.kind == "d" and t.a is ds)]
            self._deps(o, reads, writes)
            for b, lw in saved:
                b.lw = lw
        else:
            self._deps(o, reads, writes)
        lst.append(o)
        ds.issued += 16
        tok = Tok("d", ds, ds.issued)
        for b in reads:
            b.rd.append(tok)
        for b in writes:
            if partial:
                b.lw = [t for t in b.lw if t.kind == "d" and t.a is ds] + [tok]
            else:
                b.lw = [tok]
                b.rd = []
        if final:
            self.final_tokens.append(tok)
        return o

    def dma_like(self, queue, fn, reads=(), writes=(), sem_buf=None, final=False, inc=16):
        if sem_buf is None:
            sem_buf = writes[0] if writes else reads[0]
        if sem_buf.dsem is None:
            sem_buf.dsem = DmaSem(sem_buf.name)
            self.dsems.append(sem_buf.dsem)
        ds = sem_buf.dsem
        lst = self.ops[queue]
        o = Op(queue, fn, len(lst), is_dma=True, dsem=ds)
        o.inc = inc
        self._deps(o, reads, writes)
        lst.append(o)
        ds.issued += inc
        tok = Tok("d", ds, ds.issued)
        for b in reads:
            b.rd.append(tok)
        for b in writes:
            b.lw = [tok]
            b.rd = []
        if final:
            self.final_tokens.append(tok)
        return o

    def emit(self):
        nc = self.nc
        with contextlib.ExitStack() as st:
            esem = {}
            for e in ENGS:
                esem[e] = st.enter_context(nc.semaphore("s_" + e))
            for i, ds in enumerate(self.dsems):
                ds.handle = st.enter_context(nc.semaphore("d%d" % i))
            cnt = {}
            for e in ENGS:
                c = 0
                arr = []
                for o in self.ops[e]:
                    if o.signal and not o.is_dma:
                        c += 1
                    arr.append(c)
                cnt[e] = arr

            def resolve(t):
                if t.kind == "e":
                    return esem[t.a], cnt[t.a][t.b], ("e", t.a)
                return t.a.handle, t.b, ("d", id(t.a))

            block = st.enter_context(nc.Block())
            handles = {"pe": block.tensor, "act": block.scalar, "dve": block.vector,
                       "pool": block.gpsimd, "sp": block.sync}

            def make(e):
                def body(eng):
                    waited = {}
                    for o in self.ops[e]:
                        for t in o.waits:
                            h, v, key = resolve(t)
                            if waited.get(key, 0) >= v:
                                continue
                            waited[key] = v
                            eng.wait_ge(h, v)
                        ins = o.fn(eng)
                        if o.is_dma:
                            ins.then_inc(o.dsem.handle, o.inc)
                        elif o.signal:
                            ins.then_inc(esem[e], 1)
                    if e == "sp":
                        for t in self.final_tokens:
                            h, v, key = resolve(t)
                            if waited.get(key, 0) >= v:
                                continue
                            waited[key] = v
                            eng.wait_ge(h, v)
                return body

            for e in ENGS:
                handles[e](make(e))


D = 1024
DEPTH = 2
SEQ = 256
LS = 2048
GRID_W = 64
EPS = 1e-6
NEG = -1e30
ATT_SCALE = 0.125
MLA_SCALE = 96 ** -0.5
D_FF = 2752
NV = 288
NB = 512
PADP = 16
NA_CHUNKS = [list(range(0, 6)), list(range(2, 10)), list(range(6, 14)), list(range(10, 16))]
NA_TILE0 = [0, 6, 14, 22]
SWA_WIN = [(0, 0, 511), (510, 511, 1021), (1020, 1021, 1531), (1530, 1531, 2041), (2032, 2041, 2048)]
ST_W = 928


def na_qrange(blk, c):
    rows = []
    for r in range(8 * blk, 8 * blk + 8):
        st_ = min(max(r - 4, 0), 24)
        if any(st_ <= kr < st_ + 8 for kr in (2 * c, 2 * c + 1)):
            rows.append(r)
    return ((rows[0] - 8 * blk) * 64, (rows[-1] + 1 - 8 * blk) * 64)


def swa_chunks(i):
    return [c for c in range(4 * i - 1, 4 * i + 5) if 0 <= c < 16]


def build_nc(do_sample=True, do_prompt=True, dbg_layers=DEPTH, dbg_skipC=False, dbg_skipB=False):
    nc = bass.Bass("TRN2", target_bir_lowering=False)

    def din(name, shape):
        return nc.dram_tensor(name, list(shape), F32, kind="ExternalInput").ap()

    def dout(name, shape):
        return nc.dram_tensor(name, list(shape), F32, kind="ExternalOutput").ap()

    xpT = din("xpT", [D, 1024])
    xsT = din("xsT", [D, LS])
    cvec = din("cvec", [128, 16])
    vecs = din("vecs", [DEPTH, 128, NV])
    w_mod = din("w_mod", [DEPTH, D, 6 * D])
    w_in = din("w_in", [DEPTH, D, 1888])
    w_x = din("w_x", [DEPTH, D, 672])
    w_gate = din("w_gate", [DEPTH, D, 4 * D])
    w_branch = din("w_branch", [DEPTH, D, D])
    w_out = din("w_out", [DEPTH, D, D])
    w_up = din("w_up", [DEPTH, D, 2 * D_FF])
    w_down = din("w_down", [DEPTH, D_FF, D])
    w_uq = din("w_uq", [DEPTH, 192, 384])
    w_uqp = din("w_uqp", [DEPTH, 192, 128])
    w_ukv = din("w_ukv", [DEPTH, 128, 512])
    poolbd = din("poolbd", [DEPTH, 256, 128])
    cna_k = din("cna_k", [DEPTH, 256, 256])
    cna_v = din("cna_v", [DEPTH, 256, 256])
    cckv = din("cckv", [DEPTH, 256, 128])
    ckrope = din("ckrope", [DEPTH, 256, 32])
    csw_k = din("csw_k", [DEPTH, 256, 128])
    csw_v = din("csw_v", [DEPTH, 256, 128])
    cos64 = din("cos64", [128, LS])
    sin64 = din("sin64", [128, LS])
    cos32 = din("cos32", [32, LS])
    sin32 = din("sin32", [32, LS])
    nabias = din("nabias", [DEPTH, 4, 28, 128, 512])
    swmask = din("swmask", [6, 128, 512])

    ypT = dout("ypT", [D, 1024])
    ysT = dout("ysT", [D, LS])
    st_o = dout("st", [DEPTH, 1024, ST_W])
    XM = nc.dram_tensor("xm_scr", [D, LS], F32, kind="Internal").ap()
    X1 = nc.dram_tensor("x1_scr", [D, LS], F32, kind="Internal").ap()

    es = contextlib.ExitStack()
    with es:
        def sb(name, shape, dt):
            return es.enter_context(nc.sbuf_tensor(name, list(shape), dt))

        S = Sched(nc)
        _bn = [0]

        def B(name="b"):
            _bn[0] += 1
            return Buf("%s%d" % (name, _bn[0]))

        def chunks_of(v):
            return v.rearrange("(k p) t -> p k t", p=128)

        ones_bf = sb("ones_bf", [128, 128], BF16)
        id_f = sb("id_f", [128, 128], F32)
        id_bf = sb("id_bf", [128, 128], BF16)
        vec_t = sb("vec_t", [128, DEPTH, NV], F32)
        cv_t = sb("cv_t", [128, 8, 2], F32)
        sc_bf = sb("sc_bf", [128, 8, 2], BF16)
        mod_t = sb("mod_t", [128, DEPTH, 48, 2], F32)
        der_t = sb("der_t", [128, DEPTH, 2, 6, 8], F32)
        esink = sb("esink", [128, DEPTH, 4], F32)
        B_const = B("const")
        B_vec, B_cv, B_scbf, B_mod, B_der, B_esink = B(), B(), B(), B(), B(), B()

        x_t = sb("x_t", [128, 8, NB], F32)
        B_x = [B("x") for _ in range(8)]
        h_t = sb("h_t", [128, 8, NB], BF16)
        B_h = [B("h") for _ in range(8)]
        sq_t = [sb("sq%d" % i, [128, NB], BF16) for i in range(2)]
        B_sq = [B("sq") for _ in range(2)]
        rstd_t = sb("rstd_t", [128, NB], F32)
        B_rstd = B("rstd")
        NTMP = 5
        tmp_t = [sb("tmp%d" % i, [128, NB], F32) for i in range(NTMP)]
        B_tmp = [B("tmp") for _ in range(NTMP)]
        _tmpi = [0]

        def TMP():
            i = _tmpi[0] % NTMP
            _tmpi[0] += 1
            return tmp_t[i], B_tmp[i]

        KnaT = sb("KnaT", [128, 2, LS], BF16)
        KswT = sb("KswT", [128, LS], BF16)
        KA = sb("KA", [128, 4, LS], BF16)
        VV = sb("VV", [128, 2, 16, 384], BF16)
        Vna = VV[:, 0]
        Vml = VV[:, 1]
        act_t = VV[:].rearrange("p a c x -> p (a c x)")[:, 0:11264].rearrange("p (k t) -> p k t", k=22)
        B_act = [B("act") for _ in range(22)]
        Vsw = sb("Vsw", [128, 16, 192], BF16)
        pd_t = sb("pd_t", [128, 2, LS + 2 * PADP], F32)
        B_KnaT = [B("knaT") for _ in range(16)]
        B_KswT = [B("kswT") for _ in range(16)]
        B_KA = [B("KA") for _ in range(16)]
        B_Vna = [B("vna") for _ in range(16)]
        B_Vml = [B("vml") for _ in range(16)]
        B_Vsw = [B("vsw") for _ in range(16)]
        B_pd = B("pd")
        KnaTc = sb("KnaTc", [128, 2, 256], BF16)
        KswTc = sb("KswTc", [128, 256], BF16)
        KAc = sb("KAc", [128, 4, 256], BF16)
        Vnac = sb("Vnac", [128, 2, 384], BF16)
        Vmlc = sb("Vmlc", [128, 2, 384], BF16)
        Vswc = sb("Vswc", [128, 2, 192], BF16)
        B_ctx = B("ctx")
        ckvn_b = sb("ckvn_b", [128, NB], BF16)
        B_ckvn = B("ckvn")
        qna = sb("qna", [128, 2, NB], BF16)
        qsw = sb("qsw", [128, 2, NB], BF16)
        QA = sb("QA", [128, 4, NB], BF16)
        cqn = sb("cqn", [128, 2, NB], BF16)
        B_qna, B_qsw, B_QA, B_cqn = B(), B(), B(), B()
        br_t = sb("br_t", [128, 8, NB], BF16)
        B_br = [B("br") for _ in range(8)]
        NPT = 4
        pt_t = [sb("pt%d" % i, [128, NB], BF16) for i in range(NPT)]
        B_pt = [B("pt") for _ in range(NPT)]
        NBIAS = 4
        bias_t = [sb("bias%d" % i, [128, NB], BF16) for i in range(NBIAS)]
        B_bias = [B("bias") for _ in range(NBIAS)]
        _bi = [0]
        ropecst = sb("ropecst", [128, 4, NB], F32)
        rope_c64 = ropecst[:, 0]
        rope_s64 = ropecst[:, 1]
        rope_c32 = ropecst[:, 2]
        rope_s32 = ropecst[:, 3]
        cst_t = ropecst[:].rearrange("p a t -> p (a t)")[:, 0:ST_W]
        B_rope = B("rope")
        B_cst = B_rope
        mg_f = sb("mg_f", [128, 8, NB], F32)
        B_wk = [B("wk") for _ in range(8)]
        B_ov = [B("ov") for _ in range(8)]
        o_v = pd_t[:].rearrange("p c t -> p (c t)")[:, 0:8 * NB].rearrange("p (k t) -> p k t", k=8)
        pw_t = [sb("pw%d" % i, [128, NB + 2 * PADP], F32) for i in range(2)]
        B_pw = [B("pw") for _ in range(2)]
        dl_t = sb("dl_t", [128, NB], BF16)
        B_dl = B("dl")
        ld_t = sb("ld_t", [128, 2, 256], F32)
        ldb_t = sb("ldb_t", [128, 2, 256], BF16)
        B_ld, B_ldb = B("ld"), B("ldb")

        psg = [es.enter_context(nc.psum_tensor("psg%d" % i, [128, 512], F32)) for i in range(5)]
        B_psg = [B("psg") for _ in range(5)]
        pso = [es.enter_context(nc.psum_tensor("pso%d" % i, [128, 512], F32)) for i in range(2)]
        B_pso = [B("pso") for _ in range(2)]
        pst = es.enter_context(nc.psum_tensor("pst", [128, 1024], BF16))
        B_pst = B("pst")
        _pi = [0, 0]

        _wide = [False]

        def PSG():
            if _wide[0]:
                i = _pi[0] % 7
                _pi[0] += 1
                return (psg + pso)[i], (B_psg + B_pso)[i]
            i = _pi[0] % 5
            _pi[0] += 1
            return psg[i], B_psg[i]

        def PSO():
            i = _pi[1] % 2
            _pi[1] += 1
            return pso[i], B_pso[i]

        NSLOT = 3
        SLOTE = 4096
        ws = [sb("ws%d" % i, [128, SLOTE], BF16) for i in range(NSLOT)]
        B_ws = [B("ws") for _ in range(NSLOT)]
        _wi = [0]

        def wload(parts):
            i = _wi[0] % NSLOT
            _wi[0] += 1
            t, b = ws[i], B_ws[i]
            first = True
            for (off, ap, np_, nk, ncols) in parts:
                if nk == 0:
                    dst = t[0:np_, off:off + ncols]
                else:
                    dst = t[0:np_, off:off + nk * ncols].rearrange("p (k c) -> p k c", k=nk)
                S.dma("pool", dst, ap, writes=[b], partial=not first)
                first = False
            return t, b

        def wslab(w2d, c0, ncols, off=0):
            return (off, w2d[:, c0:c0 + ncols].rearrange("(k p) c -> p k c", p=128), 128, 8, ncols)

        _alt = [0]

        def evac_eng():
            _alt[0] += 1
            return "act" if _alt[0] % 2 else "dve"

        def copy_op(eng, out, in_, reads, writes, scale=None):
            if eng == "act":
                if scale is None:
                    S.op("act", lambda e: e.copy(out=out, in_=in_), reads=reads, writes=writes)
                else:
                    S.op("act", lambda e: e.mul(out=out, in_=in_, mul=scale), reads=reads, writes=writes)
            else:
                if scale is None:
                    S.op("dve", lambda e: e.tensor_copy(out=out, in_=in_), reads=reads, writes=writes)
                else:
                    S.op("dve", lambda e: e.tensor_scalar_mul(out=out, in0=in_, scalar1=scale), reads=reads, writes=writes)

        def mm(out, lhsT, rhs, start, stop, reads, writes):
            S.op("pe", lambda e: e.matmul(out, lhsT, rhs, start=start, stop=stop), reads=reads, writes=writes)


        def ACT(out, in_, func, reads, writes, bias=None, scale=None, accum=None):
            kw = {}
            if bias is not None:
                kw["bias"] = bias
            if scale is not None:
                kw["scale"] = scale
            if accum is not None:
                kw["accum_out"] = accum
            S.op("act", lambda e: e.activation(out=out, in_=in_, func=func, **kw), reads=reads, writes=writes)

        def ACOPY(out, in_, reads, writes):
            S.op("act", lambda e: e.copy(out=out, in_=in_), reads=reads, writes=writes)

        def AMUL(out, in_, mul, reads, writes):
            S.op("act", lambda e: e.mul(out=out, in_=in_, mul=mul), reads=reads, writes=writes)

        def TT(out, in0, in1, op, reads, writes):
            S.op("dve", lambda e: e.tensor_tensor(out=out, in0=in0, in1=in1, op=op), reads=reads, writes=writes)

        def STT(out, in0, scalar, in1, op0, op1, reads, writes):
            S.op("dve", lambda e: e.scalar_tensor_tensor(out=out, in0=in0, scalar=scalar, in1=in1, op0=op0, op1=op1),
                 reads=reads, writes=writes)

        def TSMUL(out, in0, sc, reads, writes):
            S.op("dve", lambda e: e.tensor_scalar_mul(out=out, in0=in0, scalar1=sc), reads=reads, writes=writes)

        def TSADD(out, in0, sc, reads, writes):
            S.op("dve", lambda e: e.tensor_scalar_add(out=out, in0=in0, scalar1=sc), reads=reads, writes=writes)

        def VCOPY(out, in_, reads, writes):
            S.op("dve", lambda e: e.tensor_copy(out=out, in_=in_), reads=reads, writes=writes)

        def RECIP(out, in_, reads, writes):
            S.op("dve", lambda e: e.reciprocal(out=out, in_=in_), reads=reads, writes=writes)

        def MEMSET(ap, val, writes, reads=()):
            S.op("dve", lambda e: e.memset(ap, val), reads=reads, writes=writes)

        def xfer(olds, news):
            toks = []
            for b_ in olds:
                toks.extend(b_.lw)
                toks.extend(b_.rd)
            for b_ in news:
                b_.rd.extend(toks)

        MEMSET(ones_bf[:], 1.0, [B_const])
        MEMSET(id_f[:], 0.0, [B_const])
        S.op("pool", lambda e: e.affine_select(out=id_f[:], in_=id_f[:], pattern=[[-1, 128]], compare_op=ALU.not_equal,
                                               fill=1.0, base=0, channel_multiplier=1), reads=[B_const], writes=[B_const])
        VCOPY(id_bf[:], id_f[:], [B_const], [B_const])
        MEMSET(Vsw[:, :, 64:128], 1.0, B_Vsw)
        MEMSET(Vnac[:, :, 64:128], 1.0, [B_ctx])
        MEMSET(Vmlc[:, :, 64:128], 1.0, [B_ctx])
        MEMSET(Vnac[:, :, 256:320], 1.0, [B_ctx])
        MEMSET(Vmlc[:, :, 256:320], 1.0, [B_ctx])
        MEMSET(Vswc[:, :, 64:128], 1.0, [B_ctx])
        MEMSET(pd_t[:], 0.0, [B_pd])
        MEMSET(KA[64:128, :, :], 0.0, B_KA)
        MEMSET(KAc[64:128, :, :], 0.0, [B_ctx])
        MEMSET(QA[64:128, :, :], 0.0, [B_QA])
        S.dma("sp", vec_t[:, :, :], vecs.rearrange("l p v -> p l v"), writes=[B_vec])
        S.dma("sp", cv_t[:].rearrange("p k c -> p (k c)"), cvec, writes=[B_cv])
        ACT(sc_bf[:], cv_t[:], AF.Silu, [B_cv], [B_scbf])
        ACT(esink[:], vec_t[:, :, 249:253], AF.Exp, [B_vec], [B_esink])

        def V(l, c):
            return vec_t[:, l, c:c + 1]

        eps_ap = vec_t[:, 0, 255:256]

        def compute_mods(l):
            for s in range(12):
                t, b = wload([wslab(w_mod[l], 512 * s, 512)])
                for m4 in range(4):
                    ps, bp = PSG()
                    for k in range(8):
                        mm(ps[:, 0:2], t[:, k * 512 + m4 * 128:k * 512 + m4 * 128 + 128], sc_bf[:, k, :],
                           k == 0, k == 7, [b, B_scbf], [bp])
                    mi = 4 * s + m4
                    TSADD(mod_t[:, l, mi, :], ps[:, 0:2], V(l, 32 + mi), [bp, B_vec], [B_mod])
            for cv in range(2):
                md = lambda i0: mod_t[:, l, i0:i0 + 8, cv]
                dd = der_t[:, l, cv]
                STT(dd[:, 0, :], md(8), 1.0, vec_t[:, l, 0:8], ALU.add, ALU.mult, [B_mod, B_vec], [B_der])
                VCOPY(dd[:, 1, :], md(0), [B_mod], [B_der])
                TT(dd[:, 2, :], md(16), vec_t[:, l, 8:16], ALU.mult, [B_mod, B_vec], [B_der])
                STT(dd[:, 3, :], md(32), 1.0, vec_t[:, l, 16:24], ALU.add, ALU.mult, [B_mod, B_vec], [B_der])
                VCOPY(dd[:, 4, :], md(24), [B_mod], [B_der])
                TT(dd[:, 5, :], md(40), vec_t[:, l, 24:32], ALU.mult, [B_mod, B_vec], [B_der])

        def DER(l, cv, i, k):
            return der_t[:, l, cv, i, k:k + 1]

        def rstd_of(srcs, n, dfeat, reads):
            ps, bp = PSG()
            for i, (ap, pk) in enumerate(srcs):
                j = i % 2
                rd = reads[i] if isinstance(reads[0], list) else reads
                if j == 0:
                    ACT(sq_t[j][0:pk, 0:n], ap, AF.Square, rd, [B_sq[j]])
                else:
                    TT(sq_t[j][0:pk, 0:n], ap, ap, ALU.mult, rd, [B_sq[j]])
                mm(ps[:, 0:n], ones_bf[0:pk, :], sq_t[j][0:pk, 0:n], i == 0, i == len(srcs) - 1, [B_sq[j], B_const], [bp])
            ACT(rstd_t[:, 0:n], ps[:, 0:n], AF.Ln, [bp, B_vec], [B_rstd], bias=eps_ap, scale=1.0 / dfeat)
            ACT(rstd_t[:, 0:n], rstd_t[:, 0:n], AF.Exp, [B_rstd], [B_rstd], scale=-0.5)

        cur_x = [None, None]

        def norm_to_h(l, cv, which, n):
            x_t, B_x = cur_x
            rstd_of([(x_t[:, k, 0:n], 128) for k in range(8)], n, D, [[B_x[k]] for k in range(8)])
            for k in range(8):
                tt, bt = TMP()
                TT(tt[:, 0:n], x_t[:, k, 0:n], rstd_t[:, 0:n], ALU.mult, [B_x[k], B_rstd], [bt])
                ACT(h_t[:, k, 0:n], tt[:, 0:n], AF.Identity, [bt, B_der], [B_h[k]], bias=DER(l, cv, which + 1, k), scale=DER(l, cv, which, k))

        def post_norm_residual(l, cv, which, o_f, B_o, n):
            x_t, B_x = cur_x
            rstd_of([(o_f[:, k, 0:n], 128) for k in range(8)], n, D, [[B_o[k]] for k in range(8)])
            for k0 in (0, 4):
                tts = []
                for k in range(k0, k0 + 4):
                    tt, bt = TMP()
                    TT(tt[:, 0:n], o_f[:, k, 0:n], rstd_t[:, 0:n], ALU.mult, [B_o[k], B_rstd], [bt])
                    tts.append((tt, bt))
                for i_, k in enumerate(range(k0, k0 + 4)):
                    tt, bt = tts[i_]
                    STT(x_t[:, k, 0:n], tt[:, 0:n], DER(l, cv, which, k), x_t[:, k, 0:n], ALU.mult, ALU.add, [bt, B_der, B_x[k]], [B_x[k]])

        def fm_proj(t, b, off, ncols_slab, m0, mw, n, nk=8, krows=None, rhs=None, rreads=None, pout=0):
            ps, bp = PSG()
            for k in range(nk):
                kr = 128 if krows is None else krows[k]
                r = h_t[0:kr, k, 0:n] if rhs is None else rhs(k, kr)
                mm(ps[pout:pout + mw, 0:n], t[0:kr, off + k * ncols_slab + m0: off + k * ncols_slab + m0 + mw], r,
                   k == 0, k == nk - 1, [b] + (rreads if rreads is not None else [B_h[k]]), [bp])
            return ps, bp

        def load_rope(tok0, n):
            S.dma("sp", rope_c64[:, 0:n], cos64[:, tok0:tok0 + n], writes=[B_rope])
            S.dma("sp", rope_s64[:, 0:n], sin64[:, tok0:tok0 + n], writes=[B_rope], partial=True)
            S.dma("sp", rope_c32[64:96, 0:n], cos32[:, tok0:tok0 + n], writes=[B_rope], partial=True)
            S.dma("sp", rope_s32[64:96, 0:n], sin32[:, tok0:tok0 + n], writes=[B_rope], partial=True)

        def rope_apply(out, ps_a, bp_a, ps_b, bp_b, np_, n, c_t, s_t, wbufs, p0=0):
            t1, b1 = TMP()
            t2, b2 = TMP()
            p1 = p0 + np_
            TT(t1[p0:p1, 0:n], ps_a[p0:p1, 0:n], c_t[p0:p1, 0:n], ALU.mult, [bp_a, B_rope], [b1])
            TT(t2[p0:p1, 0:n], ps_b[p0:p1, 0:n], s_t[p0:p1, 0:n], ALU.mult, [bp_b, B_rope], [b2])
            TT(out, t1[p0:p1, 0:n], t2[p0:p1, 0:n], ALU.add, [b1, b2], wbufs)

        _pti = [0]

        def attend(q_ap, q2_ap, q_reads, chunks, nq, out_ap, out_part, y_part, scale, sink_ap, bwrite):
            po, bpo = PSO()
            nch = len(chunks)
            bl = [ci for ci, ch in enumerate(chunks) if ch.get("bias_src") is not None]
            base = _bi[0]
            _bi[0] += len(bl)
            st_ = {"nb": 0}

            def issue_bias(j):
                if j < len(bl):
                    ch_ = chunks[bl[j]]
                    bi_ = (base + j) % NBIAS
                    qa_, qb_ = ch_.get("qr", (0, nq))
                    S.dma("pool", bias_t[bi_][:, qa_:qb_], ch_["bias_src"][:, qa_:qb_], writes=[B_bias[bi_]])
                    ch_["bias"] = bias_t[bi_][:, 0:nq]
                    ch_["breads"] = [B_bias[bi_]]
            for j in range(NBIAS - 1):
                issue_bias(j)

            def s_stage(ci):
                ch = chunks[ci]
                ps, bp = PSG()
                has2 = ch.get("k2") is not None
                hasb = ch.get("bias_src") is not None
                qa, qb = ch.get("qr", (0, nq))
                mm(ps[:, qa:qb], ch["k"], q_ap[:, qa:qb], True, not (has2 or hasb), ch["reads"] + q_reads, [bp])
                if has2:
                    mm(ps[:, qa:qb], ch["k2"], q2_ap[:, qa:qb], False, not hasb, ch["reads"] + q_reads, [bp])
                if hasb:
                    mm(ps[:, qa:qb], id_bf[:, :], ch["bias"][:, qa:qb], False, True, ch["breads"] + [B_const], [bp])
                    issue_bias(st_["nb"] + NBIAS - 1)
                    st_["nb"] += 1
                return ps, bp

            BATCH = 2
            groups = [list(range(i, min(i + BATCH, nch))) for i in range(0, nch, BATCH)]
            pend = {}
            for ci in groups[0]:
                pend[ci] = s_stage(ci)
            for gi, grp_ in enumerate(groups):
                if gi + 1 < len(groups):
                    for ci in groups[gi + 1]:
                        pend[ci] = s_stage(ci)
                pts = {}
                for ci in grp_:
                    ch = chunks[ci]
                    ps, bp = pend.pop(ci)
                    pi = _pti[0] % NPT
                    _pti[0] += 1
                    pt, bpt = pt_t[pi], B_pt[pi]
                    qa, qb = ch.get("qr", (0, nq))
                    ACT(pt[:, qa:qb], ps[:, qa:qb], AF.Exp, [bp], [bpt], scale=scale)
                    pts[ci] = (pt, bpt)
                for ci in grp_:
                    ch = chunks[ci]
                    pt, bpt = pts[ci]
                    qa, qb = ch.get("qr", (0, nq))
                    mm(po[:, qa:qb], ch["v"], pt[:, qa:qb], ci == 0, ci == nch - 1, ch["reads"] + [bpt], [bpo])
            d_part = 64 - y_part
            dt_, bd = TMP()
            o0, o1 = out_part, out_part + 64
            if sink_ap is None:
                ACOPY(dt_[o0:o1, 0:nq], po[d_part:d_part + 64, 0:nq], [bpo], [bd])
            else:
                ACT(dt_[o0:o1, 0:nq], po[d_part:d_part + 64, 0:nq], AF.Identity, [bpo, B_esink], [bd],
                    bias=sink_ap[d_part:d_part + 64, :], scale=1.0)
            RECIP(dt_[o0:o1, 0:nq], dt_[o0:o1, 0:nq], [bd], [bd])
            if y_part == out_part:
                TT(out_ap, po[y_part:y_part + 64, 0:nq], dt_[o0:o1, 0:nq], ALU.mult, [bpo, bd], bwrite)
            else:
                yt, by = TMP()
                ACOPY(yt[o0:o1, 0:nq], po[y_part:y_part + 64, 0:nq], [bpo], [by])
                TT(out_ap, yt[o0:o1, 0:nq], dt_[o0:o1, 0:nq], ALU.mult, [by, bd], bwrite)

        def vpair_dst(vt, cidx, blk3):
            return vt[:, cidx, :].rearrange("p (a b x) -> p a b x", a=2, b=3)[:, :, blk3, :]

        def vpair_src(src256, which):
            return src256.rearrange("p (a b x) -> p a b x", a=2, b=2)[:, :, which, :]

        def load_ukv(l):
            i = _wi[0] % NSLOT
            _wi[0] += 1
            t4, b4 = ws[i], B_ws[i]
            src4 = w_ukv[l].rearrange("k (h t c) -> k t h c", h=4, t=2, c=64)
            for t_ in range(2):
                S.dma("pool", t4[:, t_ * 256:(t_ + 1) * 256].rearrange("p (h c) -> p h c", h=4), src4[:, t_], writes=[b4], partial=(t_ > 0))
            return t4, b4

        def begin_group(nck=16):
            xfer(B_act, B_Vna + B_Vml)
            MEMSET(VV[:, :, 0:nck, 64:128], 1.0, B_Vna + B_Vml)
            MEMSET(VV[:, :, 0:nck, 256:320], 1.0, B_Vna + B_Vml)

        def phase_A(l, cv, grp, tok0, n, st_row0=None):
            smp = grp == "s"
            norm_to_h(l, cv, 0, n)
            c0 = tok0 // 128
            ncks = n // 128
            wbK = lambda lst: [lst[c0 + i] for i in range(ncks)]
            if smp:
                load_rope(tok0, n)
            t1, b1 = wload([wslab(w_in[l], 256, 256)])
            for m in range(2):
                ps, bp = fm_proj(t1, b1, 0, 256, m * 128, 128, n)
                copy_op(evac_eng(), KnaT[:, m, tok0:tok0 + n], ps[:, 0:n], [bp], wbK(B_KnaT))
            t2, b2 = wload([wslab(w_in[l], 1376, 128, 0), wslab(w_x[l], 512, 128, 1024)])
            ps, bp = fm_proj(t2, b2, 0, 128, 0, 128, n)
            if smp:
                psb, bpb = fm_proj(t2, b2, 1024, 128, 0, 128, n)
                rope_apply(KswT[:, tok0:tok0 + n], ps, bp, psb, bpb, 128, n, rope_c64, rope_s64, wbK(B_KswT))
            else:
                copy_op(evac_eng(), KswT[:, tok0:tok0 + n], ps[:, 0:n], [bp], wbK(B_KswT))
            t3, b3 = wload([wslab(w_in[l], 960, 160, 0), wslab(w_x[l], 640, 32, 1280), wslab(w_in[l], 1632, 256, 1536)])
            ps, bp = fm_proj(t3, b3, 0, 160, 0, 128, n)
            ck, bck = TMP()
            ACOPY(ck[:, 0:n], ps[:, 0:n], [bp], [bck])
            rstd_of([(ck[:, 0:n], 128)], n, 128, [bck])
            tt, bt = TMP()
            TT(tt[:, 0:n], ck[:, 0:n], rstd_t[:, 0:n], ALU.mult, [bck, B_rstd], [bt])
            AMUL(ckvn_b[:, 0:n], tt[:, 0:n], V(l, 246), [bt, B_vec], [B_ckvn])
            ps, bp = fm_proj(t3, b3, 0, 160, 128, 32, n, pout=64)
            if smp:
                psb, bpb = fm_proj(t3, b3, 1280, 32, 0, 32, n, pout=64)
                rope_apply(KA[64:96, 0, tok0:tok0 + n], ps, bp, psb, bpb, 32, n, rope_c32, rope_s32, wbK(B_KA), p0=64)
                for h_ in range(1, 4):
                    VCOPY(KA[64:96, h_, tok0:tok0 + n], KA[64:96, 0, tok0:tok0 + n], wbK(B_KA), wbK(B_KA))
            else:
                for h_ in range(4):
                    copy_op(evac_eng(), KA[64:96, h_, tok0:tok0 + n], ps[64:96, 0:n], [bp], wbK(B_KA))
            for m in range(2):
                ps, bp = fm_proj(t3, b3, 1536, 256, m * 128, 128, n)
                if smp:
                    copy_op(evac_eng(), pd_t[:, m, PADP + tok0:PADP + tok0 + n], ps[:, 0:n], [bp], [B_pd])
                else:
                    for sq_ in range(n // SEQ):
                        o_ = sq_ * (SEQ + 2 * PADP) + PADP
                        copy_op(evac_eng(), pd_t[:, m, o_:o_ + SEQ], ps[:, sq_ * SEQ:(sq_ + 1) * SEQ], [bp], [B_pd])
            t4, b4 = load_ukv(l)
            for h_ in range(4):
                ps, bp = PSG()
                mm(ps[0:64, 0:n], t4[:, h_ * 64:(h_ + 1) * 64], ckvn_b[:, 0:n], True, True, [b4, B_ckvn], [bp])
                copy_op(evac_eng(), KA[0:64, h_, tok0:tok0 + n], ps[0:64, 0:n], [bp], wbK(B_KA))
            if smp:
                t5, b5 = wload([wslab(w_in[l], 512, 256, 0), wslab(w_in[l], 1504, 128, 2048)])
            else:
                t5, b5 = wload([wslab(w_in[l], 256, 512, 0)])
                t6, b6 = wload([wslab(w_in[l], 960, 160, 0), wslab(w_in[l], 1376, 256, 1280)])
            for tt_ in range(ncks):
                cidx = c0 + tt_
                hs = lambda k: h_t[:, k, tt_ * 128:(tt_ + 1) * 128]
                if smp:
                    psa, bpa = PSG()
                    for k in range(8):
                        mm(psa[:, 0:256], hs(k), t5[:, k * 256:(k + 1) * 256], k == 0, k == 7, [b5, B_h[k]], [bpa])
                    va_src = psa[:, 0:256]
                    psc, bpc = PSG()
                    for k in range(8):
                        mm(psc[:, 0:128], hs(k), t5[:, 2048 + k * 128:2048 + (k + 1) * 128], k == 0, k == 7, [b5, B_h[k]], [bpc])
                    vc_src = psc[:, 0:128]
                    rv, rc = [bpa], [bpc]
                else:
                    psa, bpa = PSG()
                    for k in range(8):
                        mm(psa[:, 0:512], hs(k), t5[:, k * 512:(k + 1) * 512], k == 0, k == 7, [b5, B_h[k]], [bpa])
                    ACOPY(cst_t[:, 0:512], psa[:, 0:512], [bpa], [B_cst])
                    psb_, bpb_ = PSG()
                    for k in range(8):
                        mm(psb_[:, 0:160], hs(k), t6[:, k * 160:(k + 1) * 160], k == 0, k == 7, [b6, B_h[k]], [bpb_])
                    VCOPY(cst_t[:, 512:672], psb_[:, 0:160], [bpb_], [B_cst])
                    psc, bpc = PSG()
                    for k in range(8):
                        mm(psc[:, 0:256], hs(k), t6[:, 1280 + k * 256:1280 + (k + 1) * 256], k == 0, k == 7, [b6, B_h[k]], [bpc])
                    ACOPY(cst_t[:, 672:928], psc[:, 0:256], [bpc], [B_cst])
                    r0 = st_row0 + tt_ * 128
                    S.dma("sp", st_o[l, r0:r0 + 128, :], cst_t[:, :], reads=[B_cst], final=True)
                    va_src = cst_t[:, 256:512]
                    vc_src = cst_t[:, 800:928]
                    rv, rc = [B_cst], [B_cst]
                VCOPY(vpair_dst(Vna, cidx, 0), vpair_src(va_src, 0), rv, [B_Vna[cidx]])
                VCOPY(vpair_dst(Vna, cidx, 2), vpair_src(va_src, 1), rv, [B_Vna[cidx]])
                ACOPY(Vsw[:, cidx, 0:64], vc_src[:, 0:64], rc, [B_Vsw[cidx]])
                ACOPY(Vsw[:, cidx, 128:192], vc_src[:, 64:128], rc, [B_Vsw[cidx]])
                psv, bpv = PSG()
                mm(psv[:, 0:256], ckvn_b[:, tt_ * 128:(tt_ + 1) * 128], t4[:, 256:512], True, True, [b4, B_ckvn], [bpv])
                VCOPY(vpair_dst(Vml, cidx, 0), vpair_src(psv[:, 0:256], 0), [bpv], [B_Vml[cidx]])
                ACOPY(vpair_dst(Vml, cidx, 2), vpair_src(psv[:, 0:256], 1), [bpv], [B_Vml[cidx]])

        def transpose_to(dst, src_bf, nparts_out, breads, bwrite):
            S.op("pe", lambda e: e.transpose(pst[0:nparts_out, 0:128], src_bf, id_bf[:, :]), reads=breads + [B_const], writes=[B_pst])
            copy_op(evac_eng(), dst, pst[0:nparts_out, 0:128], [B_pst], bwrite)

        def ctx_prep(l):
            def load_tm(src, ncol):
                S.dma("sp", ld_t[:, :, 0:ncol], src.rearrange("(c p) f -> p c f", p=128), writes=[B_ld])
            load_tm(cna_k[l], 256)
            VCOPY(ldb_t[:, :, :], ld_t[:, :, :], [B_ld], [B_ldb])
            for c in range(2):
                for m in range(2):
                    transpose_to(KnaTc[:, m, c * 128:(c + 1) * 128], ldb_t[:, c, m * 128:(m + 1) * 128], 128, [B_ldb], [B_ctx])
            load_tm(cna_v[l], 256)
            for c in range(2):
                for b_ in range(2):
                    VCOPY(vpair_dst(Vnac, c, 2 * b_), vpair_src(ld_t[:, c, :], b_), [B_ld], [B_ctx])
            load_tm(csw_k[l], 128)
            VCOPY(ldb_t[:, :, 0:128], ld_t[:, :, 0:128], [B_ld], [B_ldb])
            for c in range(2):
                transpose_to(KswTc[:, c * 128:(c + 1) * 128], ldb_t[:, c, 0:128], 128, [B_ldb], [B_ctx])
            load_tm(csw_v[l], 128)
            for c in range(2):
                VCOPY(Vswc[:, c, 0:64], ld_t[:, c, 0:64], [B_ld], [B_ctx])
                VCOPY(Vswc[:, c, 128:192], ld_t[:, c, 64:128], [B_ld], [B_ctx])
            S.dma("sp", ld_t[:, :, 64:96], ckrope[l].rearrange("(c p) f -> p c f", p=128), writes=[B_ld])
            VCOPY(ldb_t[:, :, 0:96], ld_t[:, :, 0:96], [B_ld], [B_ldb])
            for c in range(2):
                S.op("pe", lambda e, c=c: e.transpose(pst[0:96, 0:128], ldb_t[:, c, 0:96], id_bf[:, :]), reads=[B_ldb, B_const], writes=[B_pst])
                for h_ in range(4):
                    copy_op(evac_eng(), KAc[64:96, h_, c * 128:(c + 1) * 128], pst[64:96, 0:128], [B_pst], [B_ctx])
            load_tm(cckv[l], 128)
            t4, b4 = load_ukv(l)
            for c in range(2):
                ss, bs = TMP()
                jk, bj = TMP()
                MEMSET(ss[:, 500:501], 0.0, [bs])
                ACT(jk[:, 0:128], ld_t[:, c, 0:128], AF.Square, [B_ld, bs], [bs, bj], accum=ss[:, 500:501])
                ACT(ss[:, 501:502], ss[:, 500:501], AF.Sqrt, [bs, B_vec], [bs], bias=eps_ap, scale=1.0 / 128)
                RECIP(ss[:, 502:503], ss[:, 501:502], [bs], [bs])
                TSMUL(ldb_t[:, c, 0:128], ld_t[:, c, 0:128], ss[:, 502:503], [bs, B_ld], [B_ldb])
                S.op("pe", lambda e, c=c: e.transpose(pst[:, 0:128], ldb_t[:, c, 0:128], id_bf[:, :]), reads=[B_ldb, B_const], writes=[B_pst])
                AMUL(ckvn_b[:, c * 128:(c + 1) * 128], pst[:, 0:128], V(l, 246), [B_pst, B_vec], [B_ckvn])
            for h_ in range(4):
                ps, bp = PSG()
                mm(ps[0:64, 0:256], t4[:, h_ * 64:(h_ + 1) * 64], ckvn_b[:, 0:256], True, True, [b4, B_ckvn], [bp])
                copy_op(evac_eng(), KAc[0:64, h_, :], ps[0:64, 0:256], [bp], [B_ctx])
            for c in range(2):
                psv, bpv = PSG()
                mm(psv[:, 0:256], ckvn_b[:, c * 128:(c + 1) * 128], t4[:, 256:512], True, True, [b4, B_ckvn], [bpv])
                for b_ in range(2):
                    VCOPY(vpair_dst(Vmlc, c, 2 * b_), vpair_src(psv[:, 0:256], b_), [bpv], [B_ctx])

        def phase_B(l, cv, grp, blk, tok0, n):
            smp = grp == "s"
            if smp:
                load_rope(tok0, n)
            t1, b1 = wload([wslab(w_in[l], 0, 256, 0), wslab(w_x[l], 0, 256, 2048)])
            for m in range(2):
                ps, bp = fm_proj(t1, b1, 0, 256, m * 128, 128, n)
                copy_op(evac_eng(), qna[:, m, 0:n], ps[:, 0:n], [bp], [B_qna], scale=ATT_SCALE)
            if smp:
                t2, b2 = wload([wslab(w_x[l], 256, 256, 0)])
            for m in range(2):
                ps, bp = fm_proj(t1, b1, 2048, 256, m * 128, 128, n)
                if smp:
                    psb, bpb = fm_proj(t2, b2, 0, 256, m * 128, 128, n)
                    rope_apply(qsw[:, m, 0:n], ps, bp, psb, bpb, 128, n, rope_c64, rope_s64, [B_qsw])
                else:
                    copy_op(evac_eng(), qsw[:, m, 0:n], ps[:, 0:n], [bp], [B_qsw])
            t3, b3 = wload([wslab(w_in[l], 768, 192, 0)])
            cq = []
            for (m0, mw) in ((0, 128), (128, 64)):
                ps, bp = fm_proj(t3, b3, 0, 192, m0, mw, n)
                ck, bck = TMP()
                copy_op(evac_eng(), ck[0:mw, 0:n], ps[0:mw, 0:n], [bp], [bck])
                cq.append((ck, bck, mw))
            rstd_of([(ck[0:mw, 0:n], mw) for (ck, bck, mw) in cq], n, 192, [c_[1] for c_ in cq])
            for j, (ck, bck, mw) in enumerate(cq):
                tt, bt = TMP()
                TT(tt[0:mw, 0:n], ck[0:mw, 0:n], rstd_t[0:mw, 0:n], ALU.mult, [bck, B_rstd], [bt])
                AMUL(cqn[0:mw, j, 0:n], tt[0:mw, 0:n], vec_t[0:mw, l, 244 + j:245 + j], [bt, B_vec], [B_cqn])
            i = _wi[0] % NSLOT
            _wi[0] += 1
            t4, b4 = ws[i], B_ws[i]
            first = True
            for k, (r0, nr) in enumerate(((0, 128), (128, 64))):
                uq3 = w_uq[l, r0:r0 + nr, :].rearrange("k (h c) -> k h c", h=4)
                S.dma("pool", t4[0:nr, k * 512:k * 512 + 256].rearrange("p (h c) -> p h c", h=4), uq3[:, :, 0:64], writes=[b4], partial=not first)
                first = False
                S.dma("pool", t4[0:nr, k * 512 + 256:k * 512 + 384].rearrange("p (h c) -> p h c", h=4), uq3[:, :, 64:96], writes=[b4], partial=True)
                S.dma("pool", t4[0:nr, k * 512 + 384:k * 512 + 512], w_uqp[l, r0:r0 + nr, :], writes=[b4], partial=True)
            rhs_cq = lambda k, kr: cqn[0:kr, k, 0:n]
            for hh in range(4):
                ps, bp = fm_proj(t4, b4, 0, 512, hh * 64, 64, n, nk=2, krows=[128, 64], rhs=rhs_cq, rreads=[B_cqn])
                copy_op(evac_eng(), QA[0:64, hh, 0:n], ps[0:64, 0:n], [bp], [B_QA])
            for hh in range(4):
                ps, bp = fm_proj(t4, b4, 0, 512, 256 + 32 * hh, 32, n, nk=2, krows=[128, 64], rhs=rhs_cq, rreads=[B_cqn], pout=64)
                if smp:
                    psb, bpb = fm_proj(t4, b4, 0, 512, 384 + 32 * hh, 32, n, nk=2, krows=[128, 64], rhs=rhs_cq, rreads=[B_cqn], pout=64)
                    rope_apply(QA[64:96, hh, 0:n], ps, bp, psb, bpb, 32, n, rope_c32, rope_s32, [B_QA], p0=64)
                else:
                    copy_op(evac_eng(), QA[64:96, hh, 0:n], ps[64:96, 0:n], [bp], [B_QA])
            tp, bpw = wload([(0, poolbd[l].rearrange("(c p) f -> p c f", p=128), 128, 2, 128)])
            if smp:
                segs = [(PADP + tok0, 0, n, tok0 == 0, tok0 + n == LS)]
            else:
                segs = [(s_ * (SEQ + 2 * PADP) + PADP, s_ * SEQ, SEQ, True, True) for s_ in range(n // SEQ)]
            ext = [7, 6, 4, 0]
            shf = [None, 1, 2, 4]
            for c in range(2):
                for (po_, o0, ln, at_start, at_end) in segs:
                    for half in range(2):
                        g = 2 * c + half
                        p0, p1 = 64 * half, 64 * half + 64
                        src = lambda a, b_: pd_t[p0:p1, c, po_ + a:po_ + b_]
                        cur = None
                        for lev in range(g + 1):
                            a, b_ = -ext[lev], ln + ext[lev]
                            dst_t, dst_b = pw_t[lev % 2], B_pw[lev % 2]
                            dst = dst_t[p0:p1, PADP + a:PADP + b_]
                            if lev == 0:
                                TT(dst, src(a - 1, b_ - 1), src(a, b_), ALU.add, [B_pd], [dst_b])
                            else:
                                s_ = shf[lev]
                                pv_t, pv_b = pw_t[(lev - 1) % 2], B_pw[(lev - 1) % 2]
                                TT(dst, pv_t[p0:p1, PADP + a - s_:PADP + b_ - s_], pv_t[p0:p1, PADP + a + s_:PADP + b_ + s_], ALU.add,
                                   [pv_b], [dst_b])
                            cur = (dst_t, dst_b)
                        wt, wb_ = cur
                        tt, bt = TMP()
                        STT(tt[p0:p1, 0:ln], wt[p0:p1, PADP:PADP + ln], vec_t[p0:p1, 0, 253 + c:254 + c], src(0, ln), ALU.mult, ALU.subtract,
                            [wb_, B_pd, B_vec], [bt])
                        if at_start:
                            t2_, b2_ = TMP()
                            TT(t2_[p0:p1, 0:8], wt[p0:p1, PADP:PADP + 8], vec_t[p0:p1, 0, 256 + 16 * c:256 + 16 * c + 8], ALU.mult, [wb_, B_vec], [b2_])
                            TT(tt[p0:p1, 0:8], t2_[p0:p1, 0:8], src(0, 8), ALU.subtract, [b2_, B_pd, bt], [bt])
                        if at_end:
                            t2_, b2_ = TMP()
                            TT(t2_[p0:p1, 0:8], wt[p0:p1, PADP + ln - 8:PADP + ln], vec_t[p0:p1, 0, 256 + 16 * c + 8:256 + 16 * c + 16], ALU.mult,
                               [wb_, B_vec], [b2_])
                            TT(tt[p0:p1, ln - 8:ln], t2_[p0:p1, 0:8], src(ln - 8, ln), ALU.subtract, [b2_, B_pd, bt], [bt])
                        ACOPY(dl_t[p0:p1, o0:o0 + ln], tt[p0:p1, 0:ln], [bt], [B_dl])
                ps, bp = PSG()
                mm(ps[:, 0:n], tp[:, c * 128:(c + 1) * 128], dl_t[:, 0:n], True, True, [bpw, B_dl], [bp])
                AMUL(br_t[:, 6 + c, 0:n], ps[:, 0:n], V(l, 247 + c), [bp, B_vec], [B_br[6 + c]])
            if smp:
                qranges = [(0, n)]
            else:
                qranges = [(s_ * SEQ, SEQ) for s_ in range(n // SEQ)]
            for (q0, nq) in qranges:
                if smp:
                    na_list = [(c, NA_TILE0[blk] + j) for j, c in enumerate(NA_CHUNKS[blk])]
                    sw_list = swa_chunks(blk)
                    ml_list = list(range(16))
                else:
                    cc = (tok0 + q0) // 128
                    na_list = [(cc, None), (cc + 1, None)]
                    sw_list = [cc, cc + 1]
                    ml_list = [cc, cc + 1]
                for hh in range(4):
                    m, par = hh // 2, hh % 2
                    pb = 64 * par
                    vo = m * 192 + par * 64
                    chunks = []
                    if smp:
                        for c in range(2):
                            chunks.append(dict(k=KnaTc[pb:pb + 64, m, c * 128:(c + 1) * 128], v=Vnac[:, c, vo:vo + 128], reads=[B_ctx]))
                    for (c, tile_i) in na_list:
                        d = dict(k=KnaT[pb:pb + 64, m, c * 128:(c + 1) * 128], v=Vna[:, c, vo:vo + 128], reads=[B_KnaT[c], B_Vna[c]])
                        if tile_i is not None:
                            d["bias_src"] = nabias[l, hh, tile_i]
                            d["qr"] = na_qrange(blk, c)
                        chunks.append(d)
                    if smp:
                        chunks.append(chunks.pop(1))
                    attend(qna[pb:pb + 64, m, q0:q0 + nq], None, [B_qna], chunks, nq, br_t[pb:pb + 64, m, q0:q0 + nq], pb, pb,
                           1.0, None, [B_br[m]])
                    chunks = []
                    if smp:
                        for c in range(2):
                            chunks.append(dict(k=KAc[:, hh, c * 128:(c + 1) * 128], v=Vmlc[:, c, vo:vo + 128], reads=[B_ctx]))
                    for c in ml_list:
                        chunks.append(dict(k=KA[:, hh, c * 128:(c + 1) * 128], v=Vml[:, c, vo:vo + 128], reads=[B_KA[c], B_Vml[c]]))
                    attend(QA[:, hh, q0:q0 + nq], None, [B_QA], chunks, nq,
                           br_t[pb:pb + 64, 2 + m, q0:q0 + nq], pb, pb, MLA_SCALE, None, [B_br[2 + m]])
                    kv = hh // 2
                    qc_, qp = hh % 2, 64 * (hh // 2)
                    chunks = []
                    if smp:
                        for c in range(2):
                            chunks.append(dict(k=KswTc[qp:qp + 64, c * 128:(c + 1) * 128], v=Vswc[:, c, kv * 64:kv * 64 + 128], reads=[B_ctx]))
                    for c in sw_list:
                        d = dict(k=KswT[qp:qp + 64, c * 128:(c + 1) * 128], v=Vsw[:, c, kv * 64:kv * 64 + 128], reads=[B_KswT[c], B_Vsw[c]])
                        if smp:
                            d["bias_src"] = swmask[c - (4 * blk - 1)]
                            d["qr"] = (max(0, 128 * c - 128 - 512 * blk), min(512, 128 * c + 256 - 512 * blk))
                        chunks.append(d)
                    if smp:
                        chunks.append(chunks.pop(1))
                    attend(qsw[qp:qp + 64, qc_, q0:q0 + nq], None, [B_qsw], chunks, nq, br_t[pb:pb + 64, 4 + m, q0:q0 + nq], pb, 64 * kv,
                           ATT_SCALE, esink[:, l, hh:hh + 1], [B_br[4 + m]])
            for kb in range(4):
                for q4 in range(4):
                    tg, bg = wload([wslab(w_gate[l], 1024 * kb + 256 * q4, 256, 0),
                                    (2048, w_branch[l, 256 * kb:256 * kb + 256, 256 * q4:256 * q4 + 256].rearrange("(k p) c -> p k c", p=128),
                                     128, 2, 256)])
                    for m2 in range(2):
                        m = q4 * 2 + m2
                        psg_, bpg = fm_proj(tg, bg, 0, 256, m2 * 128, 128, n)
                        gt_, bgt = TMP()
                        ACT(gt_[:, 0:n], psg_[:, 0:n], AF.Sigmoid, [bpg, B_vec], [bgt], bias=V(l, 80 + 8 * kb + m), scale=1.0)
                        psp, bpp = PSG()
                        for k in range(2):
                            mm(psp[:, 0:n], tg[:, 2048 + k * 256 + m2 * 128:2048 + k * 256 + (m2 + 1) * 128], br_t[:, 2 * kb + k, 0:n], k == 0, k == 1,
                               [bg, B_br[2 * kb + k]], [bpp])
                        if kb == 0:
                            TT(mg_f[:, m, 0:n], psp[:, 0:n], gt_[:, 0:n], ALU.mult, [bpp, bgt], [B_wk[m]])
                        else:
                            TT(gt_[:, 0:n], psp[:, 0:n], gt_[:, 0:n], ALU.mult, [bpp, bgt], [bgt])
                            TT(mg_f[:, m, 0:n], mg_f[:, m, 0:n], gt_[:, 0:n], ALU.add, [bgt, B_wk[m]], [B_wk[m]])
            for m in range(8):
                copy_op(evac_eng(), h_t[:, m, 0:n], mg_f[:, m, 0:n], [B_wk[m]], [B_h[m]])
            for half in range(2):
                to, bo = wload([wslab(w_out[l], 512 * half, 512)])
                for m4 in range(4):
                    m = half * 4 + m4
                    ps, bp = fm_proj(to, bo, 0, 512, m4 * 128, 128, n)
                    copy_op(evac_eng(), mg_f[:, m, 0:n], ps[:, 0:n], [bp], [B_wk[m]])
            post_norm_residual(l, cv, 2, mg_f, B_wk, n)

        def phase_C(l, cv, n, segs):
            norm_to_h(l, cv, 3, n)
            _wide[0] = True
            prev_ = [None]

            def flush_prev():
                accs_, mw_, j_ = prev_[0]
                (aa, ba), (gg, bgg) = accs_
                ACT(gg[0:mw_, 0:n], gg[0:mw_, 0:n], AF.Silu, [bgg], [bgg])
                TT(act_t[0:mw_, j_, 0:n], aa[0:mw_, 0:n], gg[0:mw_, 0:n], ALU.mult, [ba, bgg], [B_act[j_]])
                prev_[0] = None
            xfer(B_Vna + B_Vml, B_act)
            for j2 in range(11):
                a0 = 256 * j2
                na = min(256, D_FF - a0)
                tu, bu = wload([wslab(w_up[l], a0, na, 0), wslab(w_up[l], D_FF + a0, na, 2048)])
                for jj in range((na + 127) // 128):
                    j = 2 * j2 + jj
                    mw = min(128, na - 128 * jj)
                    pp = [fm_proj(tu, bu, 2048 * part, na, 128 * jj, mw, n) for part in range(2)]
                    accs = [TMP() for part in range(2)]
                    cws = [(lambda tap, cidx=j + 22 * part: vec_t[0:mw, l, 112 + 44 * tap + cidx:113 + 44 * tap + cidx]) for part in range(2)]
                    for (s0, ln) in segs:
                        for part in range(2):
                            (ps, bp), (acc, bacc) = pp[part], accs[part]
                            AMUL(acc[0:mw, s0:s0 + ln], ps[0:mw, s0:s0 + ln], cws[part](1), [bp, B_vec], [bacc])
                    for (s0, ln) in segs:
                        for part in range(2):
                            (ps, bp), (acc, bacc) = pp[part], accs[part]
                            STT(acc[0:mw, s0 + 1:s0 + ln], ps[0:mw, s0:s0 + ln - 1], cws[part](0), acc[0:mw, s0 + 1:s0 + ln], ALU.mult, ALU.add,
                                [bp, B_vec, bacc], [bacc])
                    for (s0, ln) in segs:
                        for part in range(2):
                            (ps, bp), (acc, bacc) = pp[part], accs[part]
                            STT(acc[0:mw, s0:s0 + ln - 1], ps[0:mw, s0 + 1:s0 + ln], cws[part](2), acc[0:mw, s0:s0 + ln - 1], ALU.mult, ALU.add,
                                [bp, B_vec, bacc], [bacc])
                    if prev_[0] is not None:
                        flush_prev()
                    prev_[0] = (accs, mw, j)
            if prev_[0] is not None:
                flush_prev()
            for m in range(8):
                td, bd = wload([(0, w_down[l, 0:2688, 128 * m:128 * m + 128].rearrange("(k p) c -> p k c", p=128), 128, 21, 128),
                                (21 * 128, w_down[l, 2688:2752, 128 * m:128 * m + 128], 64, 0, 128)])
                ps, bp = PSG()
                for k in range(22):
                    kr = 128 if k < 21 else 64
                    mm(ps[:, 0:n], td[0:kr, k * 128:(k + 1) * 128], act_t[0:kr, k, 0:n], k == 0, k == 21, [bd, B_act[k]], [bp])
                if m == 0:
                    xfer([B_pd], B_ov)
                copy_op(evac_eng(), o_v[:, m, 0:n], ps[:, 0:n], [bp], [B_ov[m]])
            post_norm_residual(l, cv, 5, o_v, B_ov, n)
            _wide[0] = False
            xfer(B_ov, [B_pd])
            for (pa_, pb2_) in ((0, 16), (272, 304), (560, 576), (2064, 2080)):
                MEMSET(pd_t[:, :, pa_:pb2_], 0.0, [B_pd])

        B_XM, B_X1, B_ys = B("XM"), B("X1"), B("ys")

        XB = [(x_t, B_x), (mg_f, B_wk)]

        def load_x(src, c0, n, rb, xb=0):
            xt_, bx_ = XB[xb]
            S.dma("sp", xt_[:, :, 0:n], chunks_of(src)[:, :, c0:c0 + n], reads=rb, writes=bx_)

        def store_x(dst, c0, lo, hi, wb_, final=False, xb=0):
            xt_, bx_ = XB[xb]
            S.dma("sp", chunks_of(dst)[:, :, lo:hi], xt_[:, :, lo - c0:hi - c0], reads=bx_, writes=wb_, sem_buf=bx_[0], final=final)

        def use_x(xb):
            cur_x[0], cur_x[1] = XB[xb]

        mods_done = set()

        def need_mods(l):
            if l not in mods_done:
                mods_done.add(l)
                compute_mods(l)

        if do_sample:
            for l in range(DEPTH):
                need_mods(l)
                src = xsT if l == 0 else X1
                src_b = [] if l == 0 else [B_X1]
                begin_group()
                load_x(src, 0, NB, src_b, xb=0)
                for blk in range(4):
                    if blk + 1 < 4:
                        load_x(src, (blk + 1) * NB, NB, src_b, xb=(blk + 1) % 2)
                    use_x(blk % 2)
                    phase_A(l, 1, "s", blk * NB, NB)
                ctx_prep(l)
                use_x(0)
                for blk in range(4):
                    load_x(src, blk * NB, NB, src_b)
                    norm_to_h(l, 1, 0, NB)
                    phase_B(l, 1, "s", blk, blk * NB, NB)
                    store_x(XM, blk * NB, blk * NB, blk * NB + NB, [B_XM])
                dst = X1 if l == 0 else ysT
                dst_b = [B_X1] if l == 0 else [B_ys]
                wn = lambda w0_: min(NB, LS - w0_)
                load_x(XM, SWA_WIN[0][0], wn(SWA_WIN[0][0]), [B_XM], xb=0)
                for wi, (w0, lo, hi) in enumerate(SWA_WIN):
                    if wi + 1 < len(SWA_WIN):
                        load_x(XM, SWA_WIN[wi + 1][0], wn(SWA_WIN[wi + 1][0]), [B_XM], xb=(wi + 1) % 2)
                    use_x(wi % 2)
                    phase_C(l, 1, wn(w0), [(0, wn(w0))])
                    store_x(dst, w0, lo, hi, dst_b, final=(l == DEPTH - 1), xb=wi % 2)
                use_x(0)
        if do_prompt:
            use_x(0)
            for pb_ in range(2):
                load_x(xpT, pb_ * NB, NB, [])
                for l in range(dbg_layers):
                    need_mods(l)
                    begin_group(4)
                    phase_A(l, 0, "p", 0, NB, st_row0=pb_ * NB)
                    if not dbg_skipB:
                        phase_B(l, 0, "p", 0, 0, NB)
                    if not dbg_skipC:
                        phase_C(l, 0, NB, [(0, SEQ), (SEQ, SEQ)])
                store_x(ypT, pb_ * NB, pb_ * NB, pb_ * NB + NB, [], final=True)
        S.emit()
    return nc


def _rope_partner(d):
    h = d // 2
    hp = h // 2
    idx = np.arange(d)
    first = (idx % h) < hp
    return np.where(first, idx + hp, idx - hp), first


def _rope_tables(d, reps):
    h = d // 2
    hp = h // 2
    t = np.arange(LS)
    pos = np.stack([t // GRID_W, t % GRID_W], 0).astype(np.float32)
    idx = np.arange(d)
    sec = idx // h
    i = (idx % h) % hp
    inv = (10000.0 ** (-(np.arange(hp, dtype=np.float32)) / hp)).astype(np.float32)
    ang = pos[sec] * inv[i][:, None]
    _, first = _rope_partner(d)
    cos = np.cos(ang).astype(np.float32)
    sin = np.sin(ang).astype(np.float32) * np.where(first, -1.0, 1.0).astype(np.float32)[:, None]
    return np.tile(cos, (reps, 1)), np.tile(sin, (reps, 1))


def _chunkcols(v):
    return np.ascontiguousarray(v.reshape(-1, 128).T)


def _na_bias(rpb):
    k = np.arange(LS)
    q = np.arange(LS)
    kr, kc = k // 64, k % 64
    r, c = q // 64, q % 64
    rs = np.clip(r - 4, 0, 24)
    cs = np.clip(c - 8, 0, 48)
    valid = (kr[:, None] >= rs[None, :]) & (kr[:, None] < rs[None, :] + 8) & \
            (kc[:, None] >= cs[None, :]) & (kc[:, None] < cs[None, :] + 16)
    dr = np.clip(kr[:, None] - r[None, :] + 7, 0, 14)
    dc = np.clip(kc[:, None] - c[None, :] + 15, 0, 30)
    out = np.empty((DEPTH, 4, 28, 128, 512), np.float32)
    for blk in range(4):
        for j, ch in enumerate(NA_CHUNKS[blk]):
            ks = slice(ch * 128, ch * 128 + 128)
            qs = slice(blk * 512, blk * 512 + 512)
            g = rpb[:, :, dr[ks, qs], dc[ks, qs]]
            out[:, :, NA_TILE0[blk] + j] = np.where(valid[ks, qs][None, None], g, np.float32(NEG))
    return out


def _sw_mask():
    out = np.empty((6, 128, 512), np.float32)
    kl = np.arange(128)[:, None]
    ql = np.arange(512)[None, :]
    for i in range(6):
        delta = (i - 1) * 128
        out[i] = np.where(np.abs(ql - (kl + delta)) <= 128, 0.0, NEG)
    return out


def _prep_shared(inp):
    f = lambda a: np.ascontiguousarray(np.asarray(a, dtype=np.float32))
    sh = {}
    for k in ("w_mod", "w_in", "w_gate", "w_out", "mla_w_uq", "mla_w_ukv"):
        sh[k] = f(inp[k])
    sh["w_branch"] = f(inp["w_branch"]).reshape(DEPTH, D, D)
    sh["w_up"] = f(inp["ffn_w_up"])
    sh["w_down"] = f(inp["ffn_w_down"])
    w_in = sh["w_in"]
    p64, _ = _rope_partner(64)
    p32, _ = _rope_partner(32)
    qc = w_in[:, :, 1120:1376].reshape(DEPTH, D, 4, 64)
    order = [0, 2, 1, 3]
    qc_r = qc[:, :, order, :]
    qc_rp = qc_r[:, :, :, p64]
    kc = w_in[:, :, 1376:1504].reshape(DEPTH, D, 2, 64)
    kc_p = kc[:, :, :, p64]
    krp = w_in[:, :, 1088:1120][:, :, p32]
    sh["w_x"] = np.ascontiguousarray(np.concatenate(
        [qc_r.reshape(DEPTH, D, 256), qc_rp.reshape(DEPTH, D, 256), kc_p.reshape(DEPTH, D, 128), krp], -1))
    uq = sh["mla_w_uq"].reshape(DEPTH, 192, 4, 96)
    sh["w_uqp"] = np.ascontiguousarray(uq[:, :, :, 64:96][:, :, :, p32].reshape(DEPTH, 192, 128))
    pw = f(inp["pool_w"])
    bd = np.zeros((DEPTH, 2, 128, 128), np.float32)
    for c in range(2):
        for hf in range(2):
            bd[:, c, 64 * hf:64 * hf + 64, 64 * hf:64 * hf + 64] = pw[:, 2 * c + hf]
    sh["poolbd"] = bd.reshape(DEPTH, 256, 128)
    vecs = np.zeros((DEPTH, 128, NV), np.float32)
    for l in range(DEPTH):
        v = vecs[l]
        v[:, 0:8] = _chunkcols(f(inp["g_attn_pre"])[l])
        v[:, 8:16] = _chunkcols(f(inp["g_attn_post"])[l])
        v[:, 16:24] = _chunkcols(f(inp["g_ffn_pre"])[l])
        v[:, 24:32] = _chunkcols(f(inp["g_ffn_post"])[l])
        v[:, 32:80] = _chunkcols(f(inp["b_mod"])[l])
        v[:, 80:112] = _chunkcols(f(inp["b_gate"])[l])
        cw = f(inp["ffn_conv"])[l]
        for tap in range(3):
            for part in range(2):
                seg = np.zeros(22 * 128, np.float32)
                seg[:D_FF] = cw[tap, part * D_FF:(part + 1) * D_FF]
                v[:, 112 + 44 * tap + 22 * part:112 + 44 * tap + 22 * part + 22] = _chunkcols(seg)
        qn = np.zeros(256, np.float32)
        qn[:192] = f(inp["mla_q_norm"])[l]
        v[:, 244:246] = _chunkcols(qn)
        v[:, 246] = f(inp["mla_kv_norm"])[l]
        v[:, 247:249] = _chunkcols(f(inp["pool_scale"])[l])
        v[:, 249:253] = f(inp["swa_sink"])[l][None, :]
        wins = np.array([2, 4, 8, 16], np.float32)
        for c in range(2):
            wp = np.repeat(wins[2 * c:2 * c + 2], 64)
            v[:, 253 + c] = 1.0 / wp
            for e_ in range(8):
                t = e_
                v[:, 256 + 16 * c + e_] = 1.0 / (t + wp / 2 - np.maximum(t - wp / 2, 0))
                d_end = 8 - e_
                v[:, 256 + 16 * c + 8 + e_] = 1.0 / (np.minimum(wp / 2, d_end) + wp / 2)
        v[:, 255] = EPS
    sh["vecs"] = vecs
    c64, s64 = _rope_tables(64, 2)
    c32, s32 = _rope_tables(32, 1)
    sh["cos64"], sh["sin64"], sh["cos32"], sh["sin32"] = c64, s64, c32, s32
    sh["nabias"] = _na_bias(f(inp["na_rpb"]))
    sh["swmask"] = _sw_mask()
    return sh


_NC_CACHE = {}


def kernel(**inputs):
    f = lambda a: np.ascontiguousarray(np.asarray(a, dtype=np.float32))
    sh = _prep_shared(inputs)
    x_prompt = f(inputs["x_prompt"])
    x_sample = f(inputs["x_sample"])
    c = f(inputs["c"])
    c_ctx = f(inputs["c_ctx"])
    if "nc" not in _NC_CACHE:
        _NC_CACHE["nc"] = build_nc()
    nc = _NC_CACHE["nc"]
    in_maps = []
    for j in range(8):
        b = j // 4
        m = {}
        m["xpT"] = np.ascontiguousarray(x_prompt[4 * j:4 * j + 4].reshape(1024, D).T)
        m["xsT"] = np.ascontiguousarray(x_sample[b].T)
        cv = np.stack([_chunkcols(c_ctx), _chunkcols(c[b])], -1)
        m["cvec"] = np.ascontiguousarray(cv.reshape(128, 16))
        m["vecs"] = sh["vecs"]
        for k in ("w_mod", "w_in", "w_x", "w_gate", "w_branch", "w_out", "w_up", "w_down", "w_uqp", "poolbd",
                  "cos64", "sin64", "cos32", "sin32", "nabias", "swmask"):
            m[k] = sh[k]
        m["w_uq"] = sh["mla_w_uq"]
        m["w_ukv"] = sh["mla_w_ukv"]
        m["cna_k"] = np.ascontiguousarray(f(inputs["cache_na_k"])[b].reshape(DEPTH, 256, 256))
        m["cna_v"] = np.ascontiguousarray(f(inputs["cache_na_v"])[b].reshape(DEPTH, 256, 256))
        m["cckv"] = np.ascontiguousarray(f(inputs["cache_mla_ckv"])[b])
        m["ckrope"] = np.ascontiguousarray(f(inputs["cache_mla_krope"])[b])
        m["csw_k"] = np.ascontiguousarray(f(inputs["cache_swa_k"])[b].reshape(DEPTH, 256, 128))
        m["csw_v"] = np.ascontiguousarray(f(inputs["cache_swa_v"])[b].reshape(DEPTH, 256, 128))
        in_maps.append(m)
    res = run_bass_kernel_spmd(nc, in_maps, core_ids=list(range(8)))
    R = res.results
    y_prompt = np.concatenate([R[j]["ypT"].T.reshape(4, SEQ, D) for j in range(8)], 0).astype(np.float32)
    y_sample = np.stack([R[0]["ysT"].T, R[4]["ysT"].T], 0).astype(np.float32)
    st = np.concatenate([R[j]["st"].reshape(DEPTH, 4, SEQ, ST_W).transpose(1, 0, 2, 3) for j in range(8)], 0)
    new_na_k = np.ascontiguousarray(st[..., 0:256]).reshape(32, DEPTH, SEQ, 4, 64)
    new_na_v = np.ascontiguousarray(st[..., 256:512]).reshape(32, DEPTH, SEQ, 4, 64)
    new_ckv = np.ascontiguousarray(st[..., 512:640])
    new_kr = np.ascontiguousarray(st[..., 640:672])
    new_sk = np.ascontiguousarray(st[..., 672:800]).reshape(32, DEPTH, SEQ, 2, 64)
    new_sv = np.ascontiguousarray(st[..., 800:928]).reshape(32, DEPTH, SEQ, 2, 64)
    return (np.ascontiguousarray(y_prompt), np.ascontiguousarray(y_sample), new_na_k, new_na_v, new_ckv, new_kr, new_sk, new_sv)
```

```python
import contextlib
import numpy as np
import concourse.bass as bass
import concourse.mybir as mybir
from concourse.bass_utils import run_bass_kernel_spmd

F32 = mybir.dt.float32
BF16 = mybir.dt.bfloat16
AF = mybir.ActivationFunctionType
ALU = mybir.AluOpType

ENGS = ("pe", "act", "dve", "pool", "sp")
SAME_ENG_SYNC = {"pe": False, "act": True, "dve": True, "pool": True, "sp": False}


class Buf:
    __slots__ = ("name", "lw", "rd", "dsem")

    def __init__(self, name):
        self.name = name
        self.lw = []
        self.rd = []
        self.dsem = None


class DmaSem:
    __slots__ = ("name", "issued", "handle")

    def __init__(self, name):
        self.name = name
        self.issued = 0
        self.handle = None


class Tok:
    __slots__ = ("kind", "a", "b")

    def __init__(self, kind, a, b):
        self.kind, self.a, self.b = kind, a, b


class Op:
    __slots__ = ("eng", "fn", "waits", "signal", "idx", "dsem", "is_dma", "inc")

    def __init__(self, eng, fn, idx, is_dma=False, dsem=None):
        self.eng, self.fn, self.idx = eng, fn, idx
        self.inc = 16
        self.waits = []
        self.signal = False
        self.is_dma = is_dma
        self.dsem = dsem


class Sched:
    def __init__(self, nc):
        self.nc = nc
        self.ops = {e: [] for e in ENGS}
        self.dsems = []
        self.final_tokens = []

    def _deps(self, op, reads, writes):
        need = []
        for b in reads:
            need.extend(b.lw)
        for b in writes:
            need.extend(b.lw)
            need.extend(b.rd)
        best = {}
        for t in need:
            if t.kind == "d":
                k_ = id(t.a)
                if k_ not in best or best[k_].b < t.b:
                    best[k_] = t
        for t in need:
            if t.kind == "e":
                if t.a == op.eng and not SAME_ENG_SYNC[op.eng]:
                    continue
                self.ops[t.a][t.b].signal = True
            elif best[id(t.a)] is not t:
                continue
            op.waits.append(t)

    def op(self, eng, fn, reads=(), writes=()):
        lst = self.ops[eng]
        o = Op(eng, fn, len(lst))
        self._deps(o, reads, writes)
        lst.append(o)
        tok = Tok("e", eng, o.idx)
        for b in reads:
            b.rd.append(tok)
        for b in writes:
            b.lw = [tok]
            b.rd = []
        return o

    def dma(self, queue, out_ap, in_ap, reads=(), writes=(), sem_buf=None, partial=False, final=False):
        if sem_buf is None:
            sem_buf = writes[0] if writes else reads[0]
        if sem_buf.dsem is None:
            sem_buf.dsem = DmaSem(sem_buf.name)
            self.dsems.append(sem_buf.dsem)
        ds = sem_buf.dsem
        lst = self.ops[queue]

        def fn(e, out_ap=out_ap, in_ap=in_ap):
            return e.dma_start(out=out_ap, in_=in_ap)

        o = Op(queue, fn, len(lst), is_dma=True, dsem=ds)
        if partial:
            saved = [(b, b.lw) for b in writes]
            for b in writes:
                b.lw = [t for t in b.lw if not (t.kind == "d" and t.a is ds)]
            self._deps(o, reads, writes)
            for b, lw in saved:
                b.lw = lw
        else:
            self._deps(o, reads, writes)
        lst.append(o)
        ds.issued += 16
        tok = Tok("d", ds, ds.issued)
        for b in reads:
            b.rd.append(tok)
        for b in writes:
            if partial:
                b.lw = [t for t in b.lw if t.kind == "d" and t.a is ds] + [tok]
            else:
                b.lw = [tok]
                b.rd = []
        if final:
            self.final_tokens.append(tok)
        return o

    def dma_like(self, queue, fn, reads=(), writes=(), sem_buf=None, final=False, inc=16):
        if sem_buf is None:
            sem_buf = writes[0] if writes else reads[0]
        if sem_buf.dsem is None:
            sem_buf.dsem = DmaSem(sem_buf.name)
            self.dsems.append(sem_buf.dsem)
        ds = sem_buf.dsem
        lst = self.ops[queue]
        o = Op(queue, fn, len(lst), is_dma=True, dsem=ds)
        o.inc = inc
        self._deps(o, reads, writes)
        lst.append(o)
        ds.issued += inc
        tok = Tok("d", ds, ds.issued)
        for b in reads:
            b.rd.append(tok)
        for b in writes:
            b.lw = [tok]
            b.rd = []
        if final:
            self.final_tokens.append(tok)
        return o

    def emit(self):
        nc = self.nc
        with contextlib.ExitStack() as st:
            esem = {}
            for e in ENGS:
                esem[e] = st.enter_context(nc.semaphore("s_" + e))
            for i, ds in enumerate(self.dsems):
                ds.handle = st.enter_context(nc.semaphore("d%d" % i))
            cnt = {}
            for e in ENGS:
                c = 0
                arr = []
                for o in self.ops[e]:
                    if o.signal and not o.is_dma:
                        c += 1
                    arr.append(c)
                cnt[e] = arr

            def resolve(t):
                if t.kind == "e":
                    return esem[t.a], cnt[t.a][t.b], ("e", t.a)
                return t.a.handle, t.b, ("d", id(t.a))

            block = st.enter_context(nc.Block())
            handles = {"pe": block.tensor, "act": block.scalar, "dve": block.vector,
                       "pool": block.gpsimd, "sp": block.sync}

            def make(e):
                def body(eng):
                    waited = {}
                    for o in self.ops[e]:
                        for t in o.waits:
                            h, v, key = resolve(t)
                            if waited.get(key, 0) >= v:
                                continue
                            waited[key] = v
                            eng.wait_ge(h, v)
                        ins = o.fn(eng)
                        if o.is_dma:
                            ins.then_inc(o.dsem.handle, o.inc)
                        elif o.signal:
                            ins.then_inc(esem[e], 1)
                    if e == "sp":
                        for t in self.final_tokens:
                            h, v, key = resolve(t)
                            if waited.get(key, 0) >= v:
                                continue
                            waited[key] = v
                            eng.wait_ge(h, v)
                return body

            for e in ENGS:
                handles[e](make(e))


D = 1024
DEPTH = 2
SEQ = 256
LS = 2048
GRID_W = 64
EPS = 1e-6
NEG = -1e30
ATT_SCALE = 0.125
MLA_SCALE = 96 ** -0.5
D_FF = 2752
NV = 288
NB = 512
PADP = 16
NA_CHUNKS = [list(range(0, 6)), list(range(2, 10)), list(range(6, 14)), list(range(10, 16))]
NA_TILE0 = [0, 6, 14, 22]
SWA_WIN = [(0, 0, 511), (510, 511, 1021), (1020, 1021, 1531), (1530, 1531, 2041), (1536, 2041, 2048)]
ST_W = 928


def na_qrange(blk, c):
    rows = []
    for r in range(8 * blk, 8 * blk + 8):
        st_ = min(max(r - 4, 0), 24)
        if any(st_ <= kr < st_ + 8 for kr in (2 * c, 2 * c + 1)):
            rows.append(r)
    return ((rows[0] - 8 * blk) * 64, (rows[-1] + 1 - 8 * blk) * 64)


def swa_chunks(i):
    return [c for c in range(4 * i - 1, 4 * i + 5) if 0 <= c < 16]


def build_nc(do_sample=True, do_prompt=True, dbg_layers=DEPTH, dbg_skipC=False, dbg_skipB=False):
    nc = bass.Bass("TRN2", target_bir_lowering=False)

    def din(name, shape):
        return nc.dram_tensor(name, list(shape), F32, kind="ExternalInput").ap()

    def dout(name, shape):
        return nc.dram_tensor(name, list(shape), F32, kind="ExternalOutput").ap()

    xpT = din("xpT", [D, 1024])
    xsT = din("xsT", [D, LS])
    cvec = din("cvec", [128, 16])
    vecs = din("vecs", [DEPTH, 128, NV])
    w_mod = din("w_mod", [DEPTH, D, 6 * D])
    w_in = din("w_in", [DEPTH, D, 1888])
    w_x = din("w_x", [DEPTH, D, 672])
    w_gate = din("w_gate", [DEPTH, D, 4 * D])
    w_branch = din("w_branch", [DEPTH, D, D])
    w_out = din("w_out", [DEPTH, D, D])
    w_up = din("w_up", [DEPTH, D, 2 * D_FF])
    w_down = din("w_down", [DEPTH, D_FF, D])
    w_uq = din("w_uq", [DEPTH, 192, 384])
    w_uqp = din("w_uqp", [DEPTH, 192, 128])
    w_ukv = din("w_ukv", [DEPTH, 128, 512])
    poolbd = din("poolbd", [DEPTH, 256, 128])
    cna_k = din("cna_k", [DEPTH, 256, 256])
    cna_v = din("cna_v", [DEPTH, 256, 256])
    cckv = din("cckv", [DEPTH, 256, 128])
    ckrope = din("ckrope", [DEPTH, 256, 32])
    csw_k = din("csw_k", [DEPTH, 256, 128])
    csw_v = din("csw_v", [DEPTH, 256, 128])
    cos64 = din("cos64", [128, LS])
    sin64 = din("sin64", [128, LS])
    cos32 = din("cos32", [32, LS])
    sin32 = din("sin32", [32, LS])
    nabias = din("nabias", [DEPTH, 4, 28, 128, 512])
    swmask = din("swmask", [6, 128, 512])

    ypT = dout("ypT", [D, 1024])
    ysT = dout("ysT", [D, LS])
    st_o = dout("st", [DEPTH, 1024, ST_W])
    XM = nc.dram_tensor("xm_scr", [D, LS], F32, kind="Internal").ap()
    X1 = nc.dram_tensor("x1_scr", [D, LS], F32, kind="Internal").ap()

    es = contextlib.ExitStack()
    with es:
        def sb(name, shape, dt):
            return es.enter_context(nc.sbuf_tensor(name, list(shape), dt))

        S = Sched(nc)
        _bn = [0]

        def B(name="b"):
            _bn[0] += 1
            return Buf("%s%d" % (name, _bn[0]))

        def chunks_of(v):
            return v.rearrange("(k p) t -> p k t", p=128)

        ones_bf = sb("ones_bf", [128, 128], BF16)
        id_f = sb("id_f", [128, 128], F32)
        id_bf = sb("id_bf", [128, 128], BF16)
        vec_t = sb("vec_t", [128, DEPTH, NV], F32)
        cv_t = sb("cv_t", [128, 8, 2], F32)
        sc_bf = sb("sc_bf", [128, 8, 2], BF16)
        mod_t = sb("mod_t", [128, DEPTH, 48, 2], F32)
        der_t = sb("der_t", [128, DEPTH, 2, 6, 8], F32)
        esink = sb("esink", [128, DEPTH, 4], F32)
        B_const = B("const")
        B_vec, B_cv, B_scbf, B_mod, B_der, B_esink = B(), B(), B(), B(), B(), B()

        x_t = sb("x_t", [128, 8, NB], F32)
        B_x = [B("x") for _ in range(8)]
        h_t = sb("h_t", [128, 8, NB], BF16)
        B_h = [B("h") for _ in range(8)]
        sq_t = [sb("sq%d" % i, [128, NB], BF16) for i in range(2)]
        B_sq = [B("sq") for _ in range(2)]
        rstd_t = sb("rstd_t", [128, NB], F32)
        B_rstd = B("rstd")
        NTMP = 5
        tmp_t = [sb("tmp%d" % i, [128, NB], F32) for i in range(NTMP)]
        B_tmp = [B("tmp") for _ in range(NTMP)]
        _tmpi = [0]

        def TMP():
            i = _tmpi[0] % NTMP
            _tmpi[0] += 1
            return tmp_t[i], B_tmp[i]

        KnaT = sb("KnaT", [128, 2, LS], BF16)
        KswT = sb("KswT", [128, LS], BF16)
        KA = sb("KA", [128, 4, LS], BF16)
        VV = sb("VV", [128, 2, 16, 384], BF16)
        Vna = VV[:, 0]
        Vml = VV[:, 1]
        act_t = VV[:].rearrange("p a c x -> p (a c x)")[:, 0:11264].rearrange("p (k t) -> p k t", k=22)
        B_act = [B("act") for _ in range(22)]
        Vsw = sb("Vsw", [128, 16, 192], BF16)
        pd_t = sb("pd_t", [128, 2, LS + 2 * PADP], F32)
        B_KnaT = [B("knaT") for _ in range(16)]
        B_KswT = [B("kswT") for _ in range(16)]
        B_KA = [B("KA") for _ in range(16)]
        B_Vna = [B("vna") for _ in range(16)]
        B_Vml = [B("vml") for _ in range(16)]
        B_Vsw = [B("vsw") for _ in range(16)]
        B_pd = B("pd")
        KnaTc = sb("KnaTc", [128, 2, 256], BF16)
        KswTc = sb("KswTc", [128, 256], BF16)
        KAc = sb("KAc", [128, 4, 256], BF16)
        Vnac = sb("Vnac", [128, 2, 384], BF16)
        Vmlc = sb("Vmlc", [128, 2, 384], BF16)
        Vswc = sb("Vswc", [128, 2, 192], BF16)
        B_ctx = B("ctx")
        ckvn_b = sb("ckvn_b", [128, NB], BF16)
        B_ckvn = B("ckvn")
        qna = sb("qna", [128, 2, NB], BF16)
        qsw = sb("qsw", [128, 2, NB], BF16)
        QA = sb("QA", [128, 4, NB], BF16)
        cqn = sb("cqn", [128, 2, NB], BF16)
        B_qna, B_qsw, B_QA, B_cqn = B(), B(), B(), B()
        br_t = sb("br_t", [128, 8, NB], BF16)
        B_br = [B("br") for _ in range(8)]
        NPT = 4
        pt_t = [sb("pt%d" % i, [128, NB], BF16) for i in range(NPT)]
        B_pt = [B("pt") for _ in range(NPT)]
        NBIAS = 4
        bias_t = [sb("bias%d" % i, [128, NB], BF16) for i in range(NBIAS)]
        B_bias = [B("bias") for _ in range(NBIAS)]
        _bi = [0]
        ropecst = sb("ropecst", [128, 4, NB], F32)
        rope_c64 = ropecst[:, 0]
        rope_s64 = ropecst[:, 1]
        rope_c32 = ropecst[:, 2]
        rope_s32 = ropecst[:, 3]
        cst_t = ropecst[:].rearrange("p a t -> p (a t)")[:, 0:ST_W]
        B_rope = B("rope")
        B_cst = B_rope
        mg_f = sb("mg_f", [128, 8, NB], F32)
        B_wk = [B("wk") for _ in range(8)]
        B_ov = [B("ov") for _ in range(8)]
        o_v = pd_t[:].rearrange("p c t -> p (c t)")[:, 0:8 * NB].rearrange("p (k t) -> p k t", k=8)
        pw_t = [sb("pw%d" % i, [128, NB + 2 * PADP], F32) for i in range(2)]
        B_pw = [B("pw") for _ in range(2)]
        dl_t = sb("dl_t", [128, NB], BF16)
        B_dl = B("dl")
        ld_t = sb("ld_t", [128, 2, 256], F32)
        ldb_t = sb("ldb_t", [128, 2, 256], BF16)
        B_ld, B_ldb = B("ld"), B("ldb")

        psg = [es.enter_context(nc.psum_tensor("psg%d" % i, [128, 512], F32)) for i in range(5)]
        B_psg = [B("psg") for _ in range(5)]
        pso = [es.enter_context(nc.psum_tensor("pso%d" % i, [128, 512], F32)) for i in range(2)]
        B_pso = [B("pso") for _ in range(2)]
        pst = es.enter_context(nc.psum_tensor("pst", [128, 1024], BF16))
        B_pst = B("pst")
        _pi = [0, 0]

        _wide = [False]

        def PSG():
            if _wide[0]:
                i = _pi[0] % 7
                _pi[0] += 1
                return (psg + pso)[i], (B_psg + B_pso)[i]
            i = _pi[0] % 5
            _pi[0] += 1
            return psg[i], B_psg[i]

        def PSO():
            i = _pi[1] % 2
            _pi[1] += 1
            return pso[i], B_pso[i]

        NSLOT = 3
        SLOTE = 4096
        ws = [sb("ws%d" % i, [128, SLOTE], BF16) for i in range(NSLOT)]
        B_ws = [B("ws") for _ in range(NSLOT)]
        _wi = [0]

        def wload(parts):
            i = _wi[0] % NSLOT
            _wi[0] += 1
            t, b = ws[i], B_ws[i]
            first = True
            for (off, ap, np_, nk, ncols) in parts:
                if nk == 0:
                    dst = t[0:np_, off:off + ncols]
                else:
                    dst = t[0:np_, off:off + nk * ncols].rearrange("p (k c) -> p k c", k=nk)
                S.dma("pool", dst, ap, writes=[b], partial=not first)
                first = False
            return t, b

        def wslab(w2d, c0, ncols, off=0):
            return (off, w2d[:, c0:c0 + ncols].rearrange("(k p) c -> p k c", p=128), 128, 8, ncols)

        _alt = [0]

        def evac_eng():
            _alt[0] += 1
            return "act" if _alt[0] % 2 else "dve"

        def copy_op(eng, out, in_, reads, writes, scale=None):
            if eng == "act":
                if scale is None:
                    S.op("act", lambda e: e.copy(out=out, in_=in_), reads=reads, writes=writes)
                else:
                    S.op("act", lambda e: e.mul(out=out, in_=in_, mul=scale), reads=reads, writes=writes)
            else:
                if scale is None:
                    S.op("dve", lambda e: e.tensor_copy(out=out, in_=in_), reads=reads, writes=writes)
                else:
                    S.op("dve", lambda e: e.tensor_scalar_mul(out=out, in0=in_, scalar1=scale), reads=reads, writes=writes)

        def mm(out, lhsT, rhs, start, stop, reads, writes):
            S.op("pe", lambda e: e.matmul(out, lhsT, rhs, start=start, stop=stop), reads=reads, writes=writes)


        def ACT(out, in_, func, reads, writes, bias=None, scale=None, accum=None):
            kw = {}
            if bias is not None:
                kw["bias"] = bias
            if scale is not None:
                kw["scale"] = scale
            if accum is not None:
                kw["accum_out"] = accum
            S.op("act", lambda e: e.activation(out=out, in_=in_, func=func, **kw), reads=reads, writes=writes)

        def ACOPY(out, in_, reads, writes):
            S.op("act", lambda e: e.copy(out=out, in_=in_), reads=reads, writes=writes)

        def AMUL(out, in_, mul, reads, writes):
            S.op("act", lambda e: e.mul(out=out, in_=in_, mul=mul), reads=reads, writes=writes)

        def TT(out, in0, in1, op, reads, writes):
            S.op("dve", lambda e: e.tensor_tensor(out=out, in0=in0, in1=in1, op=op), reads=reads, writes=writes)

        def STT(out, in0, scalar, in1, op0, op1, reads, writes):
            S.op("dve", lambda e: e.scalar_tensor_tensor(out=out, in0=in0, scalar=scalar, in1=in1, op0=op0, op1=op1),
                 reads=reads, writes=writes)

        def TSMUL(out, in0, sc, reads, writes):
            S.op("dve", lambda e: e.tensor_scalar_mul(out=out, in0=in0, scalar1=sc), reads=reads, writes=writes)

        def TSADD(out, in0, sc, reads, writes):
            S.op("dve", lambda e: e.tensor_scalar_add(out=out, in0=in0, scalar1=sc), reads=reads, writes=writes)

        def VCOPY(out, in_, reads, writes):
            S.op("dve", lambda e: e.tensor_copy(out=out, in_=in_), reads=reads, writes=writes)

        def RECIP(out, in_, reads, writes):
            S.op("dve", lambda e: e.reciprocal(out=out, in_=in_), reads=reads, writes=writes)

        def MEMSET(ap, val, writes, reads=()):
            S.op("dve", lambda e: e.memset(ap, val), reads=reads, writes=writes)

        def xfer(olds, news):
            toks = []
            for b_ in olds:
                toks.extend(b_.lw)
                toks.extend(b_.rd)
            for b_ in news:
                b_.rd.extend(toks)

        MEMSET(ones_bf[:], 1.0, [B_const])
        MEMSET(id_f[:], 0.0, [B_const])
        S.op("pool", lambda e: e.affine_select(out=id_f[:], in_=id_f[:], pattern=[[-1, 128]], compare_op=ALU.not_equal,
                                               fill=1.0, base=0, channel_multiplier=1), reads=[B_const], writes=[B_const])
        VCOPY(id_bf[:], id_f[:], [B_const], [B_const])
        MEMSET(Vsw[:, :, 64:128], 1.0, B_Vsw)
        MEMSET(Vnac[:, :, 64:128], 1.0, [B_ctx])
        MEMSET(Vmlc[:, :, 64:128], 1.0, [B_ctx])
        MEMSET(Vnac[:, :, 256:320], 1.0, [B_ctx])
        MEMSET(Vmlc[:, :, 256:320], 1.0, [B_ctx])
        MEMSET(Vswc[:, :, 64:128], 1.0, [B_ctx])
        MEMSET(pd_t[:], 0.0, [B_pd])
        MEMSET(KA[64:128, :, :], 0.0, B_KA)
        MEMSET(KAc[64:128, :, :], 0.0, [B_ctx])
        MEMSET(QA[64:128, :, :], 0.0, [B_QA])
        S.dma("sp", vec_t[:, :, :], vecs.rearrange("l p v -> p l v"), writes=[B_vec])
        S.dma("sp", cv_t[:].rearrange("p k c -> p (k c)"), cvec, writes=[B_cv])
        ACT(sc_bf[:], cv_t[:], AF.Silu, [B_cv], [B_scbf])
        ACT(esink[:], vec_t[:, :, 249:253], AF.Exp, [B_vec], [B_esink])

        def V(l, c):
            return vec_t[:, l, c:c + 1]

        eps_ap = vec_t[:, 0, 255:256]

        def compute_mods(l):
            for s in range(12):
                t, b = wload([wslab(w_mod[l], 512 * s, 512)])
                for m4 in range(4):
                    ps, bp = PSG()
                    for k in range(8):
                        mm(ps[:, 0:2], t[:, k * 512 + m4 * 128:k * 512 + m4 * 128 + 128], sc_bf[:, k, :],
                           k == 0, k == 7, [b, B_scbf], [bp])
                    mi = 4 * s + m4
                    TSADD(mod_t[:, l, mi, :], ps[:, 0:2], V(l, 32 + mi), [bp, B_vec], [B_mod])
            for cv in range(2):
                md = lambda i0: mod_t[:, l, i0:i0 + 8, cv]
                dd = der_t[:, l, cv]
                STT(dd[:, 0, :], md(8), 1.0, vec_t[:, l, 0:8], ALU.add, ALU.mult, [B_mod, B_vec], [B_der])
                VCOPY(dd[:, 1, :], md(0), [B_mod], [B_der])
                TT(dd[:, 2, :], md(16), vec_t[:, l, 8:16], ALU.mult, [B_mod, B_vec], [B_der])
                STT(dd[:, 3, :], md(32), 1.0, vec_t[:, l, 16:24], ALU.add, ALU.mult, [B_mod, B_vec], [B_der])
                VCOPY(dd[:, 4, :], md(24), [B_mod], [B_der])
                TT(dd[:, 5, :], md(40), vec_t[:, l, 24:32], ALU.mult, [B_mod, B_vec], [B_der])

        def DER(l, cv, i, k):
            return der_t[:, l, cv, i, k:k + 1]

        def rstd_of(srcs, n, dfeat, reads):
            ps, bp = PSG()
            for i, (ap, pk) in enumerate(srcs):
                j = i % 2
                rd = reads[i] if isinstance(reads[0], list) else reads
                if j == 0:
                    ACT(sq_t[j][0:pk, 0:n], ap, AF.Square, rd, [B_sq[j]])
                else:
                    TT(sq_t[j][0:pk, 0:n], ap, ap, ALU.mult, rd, [B_sq[j]])
                mm(ps[:, 0:n], ones_bf[0:pk, :], sq_t[j][0:pk, 0:n], i == 0, i == len(srcs) - 1, [B_sq[j], B_const], [bp])
            ACT(rstd_t[:, 0:n], ps[:, 0:n], AF.Ln, [bp, B_vec], [B_rstd], bias=eps_ap, scale=1.0 / dfeat)
            ACT(rstd_t[:, 0:n], rstd_t[:, 0:n], AF.Exp, [B_rstd], [B_rstd], scale=-0.5)

        cur_x = [None, None]

        def norm_to_h(l, cv, which, n):
            x_t, B_x = cur_x
            rstd_of([(x_t[:, k, 0:n], 128) for k in range(8)], n, D, [[B_x[k]] for k in range(8)])
            for k in range(8):
                tt, bt = TMP()
                TT(tt[:, 0:n], x_t[:, k, 0:n], rstd_t[:, 0:n], ALU.mult, [B_x[k], B_rstd], [bt])
                ACT(h_t[:, k, 0:n], tt[:, 0:n], AF.Identity, [bt, B_der], [B_h[k]], bias=DER(l, cv, which + 1, k), scale=DER(l, cv, which, k))

        def post_norm_residual(l, cv, which, o_f, B_o, n):
            x_t, B_x = cur_x
            rstd_of([(o_f[:, k, 0:n], 128) for k in range(8)], n, D, [[B_o[k]] for k in range(8)])
            for k0 in (0, 4):
                tts = []
                for k in range(k0, k0 + 4):
                    tt, bt = TMP()
                    TT(tt[:, 0:n], o_f[:, k, 0:n], rstd_t[:, 0:n], ALU.mult, [B_o[k], B_rstd], [bt])
                    tts.append((tt, bt))
                for i_, k in enumerate(range(k0, k0 + 4)):
                    tt, bt = tts[i_]
                    STT(x_t[:, k, 0:n], tt[:, 0:n], DER(l, cv, which, k), x_t[:, k, 0:n], ALU.mult, ALU.add, [bt, B_der, B_x[k]], [B_x[k]])

        def fm_proj(t, b, off, ncols_slab, m0, mw, n, nk=8, krows=None, rhs=None, rreads=None, pout=0):
            ps, bp = PSG()
            for k in range(nk):
                kr = 128 if krows is None else krows[k]
                r = h_t[0:kr, k, 0:n] if rhs is None else rhs(k, kr)
                mm(ps[pout:pout + mw, 0:n], t[0:kr, off + k * ncols_slab + m0: off + k * ncols_slab + m0 + mw], r,
                   k == 0, k == nk - 1, [b] + (rreads if rreads is not None else [B_h[k]]), [bp])
            return ps, bp

        def load_rope(tok0, n):
            S.dma("sp", rope_c64[:, 0:n], cos64[:, tok0:tok0 + n], writes=[B_rope])
            S.dma("sp", rope_s64[:, 0:n], sin64[:, tok0:tok0 + n], writes=[B_rope], partial=True)
            S.dma("sp", rope_c32[64:96, 0:n], cos32[:, tok0:tok0 + n], writes=[B_rope], partial=True)
            S.dma("sp", rope_s32[64:96, 0:n], sin32[:, tok0:tok0 + n], writes=[B_rope], partial=True)

        def rope_apply(out, ps_a, bp_a, ps_b, bp_b, np_, n, c_t, s_t, wbufs, p0=0):
            t1, b1 = TMP()
            t2, b2 = TMP()
            p1 = p0 + np_
            TT(t1[p0:p1, 0:n], ps_a[p0:p1, 0:n], c_t[p0:p1, 0:n], ALU.mult, [bp_a, B_rope], [b1])
            TT(t2[p0:p1, 0:n], ps_b[p0:p1, 0:n], s_t[p0:p1, 0:n], ALU.mult, [bp_b, B_rope], [b2])
            TT(out, t1[p0:p1, 0:n], t2[p0:p1, 0:n], ALU.add, [b1, b2], wbufs)

        _pti = [0]

        def attend(q_ap, q2_ap, q_reads, chunks, nq, out_ap, out_part, y_part, scale, sink_ap, bwrite):
            po, bpo = PSO()
            nch = len(chunks)
            bl = [ci for ci, ch in enumerate(chunks) if ch.get("bias_src") is not None]
            base = _bi[0]
            _bi[0] += len(bl)
            st_ = {"nb": 0}

            def issue_bias(j):
                if j < len(bl):
                    ch_ = chunks[bl[j]]
                    bi_ = (base + j) % NBIAS
                    qa_, qb_ = ch_.get("qr", (0, nq))
                    S.dma("pool", bias_t[bi_][:, qa_:qb_], ch_["bias_src"][:, qa_:qb_], writes=[B_bias[bi_]])
                    ch_["bias"] = bias_t[bi_][:, 0:nq]
                    ch_["breads"] = [B_bias[bi_]]
            for j in range(NBIAS - 1):
                issue_bias(j)

            def s_stage(ci):
                ch = chunks[ci]
                ps, bp = PSG()
                has2 = ch.get("k2") is not None
                hasb = ch.get("bias_src") is not None
                qa, qb = ch.get("qr", (0, nq))
                mm(ps[:, qa:qb], ch["k"], q_ap[:, qa:qb], True, not (has2 or hasb), ch["reads"] + q_reads, [bp])
                if has2:
                    mm(ps[:, qa:qb], ch["k2"], q2_ap[:, qa:qb], False, not hasb, ch["reads"] + q_reads, [bp])
                if hasb:
                    mm(ps[:, qa:qb], id_bf[:, :], ch["bias"][:, qa:qb], False, True, ch["breads"] + [B_const], [bp])
                    issue_bias(st_["nb"] + NBIAS - 1)
                    st_["nb"] += 1
                return ps, bp

            BATCH = 2
            groups = [list(range(i, min(i + BATCH, nch))) for i in range(0, nch, BATCH)]
            pend = {}
            for ci in groups[0]:
                pend[ci] = s_stage(ci)
            for gi, grp_ in enumerate(groups):
                if gi + 1 < len(groups):
                    for ci in groups[gi + 1]:
                        pend[ci] = s_stage(ci)
                pts = {}
                for ci in grp_:
                    ch = chunks[ci]
                    ps, bp = pend.pop(ci)
                    pi = _pti[0] % NPT
                    _pti[0] += 1
                    pt, bpt = pt_t[pi], B_pt[pi]
                    qa, qb = ch.get("qr", (0, nq))
                    ACT(pt[:, qa:qb], ps[:, qa:qb], AF.Exp, [bp], [bpt], scale=scale)
                    pts[ci] = (pt, bpt)
                for ci in grp_:
                    ch = chunks[ci]
                    pt, bpt = pts[ci]
                    qa, qb = ch.get("qr", (0, nq))
                    mm(po[:, qa:qb], ch["v"], pt[:, qa:qb], ci == 0, ci == nch - 1, ch["reads"] + [bpt], [bpo])
            d_part = 64 - y_part
            dt_, bd = TMP()
            o0, o1 = out_part, out_part + 64
            if sink_ap is None:
                ACOPY(dt_[o0:o1, 0:nq], po[d_part:d_part + 64, 0:nq], [bpo], [bd])
            else:
                ACT(dt_[o0:o1, 0:nq], po[d_part:d_part + 64, 0:nq], AF.Identity, [bpo, B_esink], [bd],
                    bias=sink_ap[d_part:d_part + 64, :], scale=1.0)
            RECIP(dt_[o0:o1, 0:nq], dt_[o0:o1, 0:nq], [bd], [bd])
            if y_part == out_part:
                TT(out_ap, po[y_part:y_part + 64, 0:nq], dt_[o0:o1, 0:nq], ALU.mult, [bpo, bd], bwrite)
            else:
                yt, by = TMP()
                ACOPY(yt[o0:o1, 0:nq], po[y_part:y_part + 64, 0:nq], [bpo], [by])
                TT(out_ap, yt[o0:o1, 0:nq], dt_[o0:o1, 0:nq], ALU.mult, [by, bd], bwrite)

        def vpair_dst(vt, cidx, blk3):
            return vt[:, cidx, :].rearrange("p (a b x) -> p a b x", a=2, b=3)[:, :, blk3, :]

        def vpair_src(src256, which):
            return src256.rearrange("p (a b x) -> p a b x", a=2, b=2)[:, :, which, :]

        def load_ukv(l):
            i = _wi[0] % NSLOT
            _wi[0] += 1
            t4, b4 = ws[i], B_ws[i]
            src4 = w_ukv[l].rearrange("k (h t c) -> k t h c", h=4, t=2, c=64)
            for t_ in range(2):
                S.dma("pool", t4[:, t_ * 256:(t_ + 1) * 256].rearrange("p (h c) -> p h c", h=4), src4[:, t_], writes=[b4], partial=(t_ > 0))
            return t4, b4

        def begin_group(nck=16):
            xfer(B_act, B_Vna + B_Vml)
            MEMSET(VV[:, :, 0:nck, 64:128], 1.0, B_Vna + B_Vml)
            MEMSET(VV[:, :, 0:nck, 256:320], 1.0, B_Vna + B_Vml)

        def phase_A(l, cv, grp, tok0, n, st_row0=None):
            smp = grp == "s"
            norm_to_h(l, cv, 0, n)
            c0 = tok0 // 128
            ncks = n // 128
            wbK = lambda lst: [lst[c0 + i] for i in range(ncks)]
            if smp:
                load_rope(tok0, n)
            t1, b1 = wload([wslab(w_in[l], 256, 256)])
            for m in range(2):
                ps, bp = fm_proj(t1, b1, 0, 256, m * 128, 128, n)
                copy_op(evac_eng(), KnaT[:, m, tok0:tok0 + n], ps[:, 0:n], [bp], wbK(B_KnaT))
            t2, b2 = wload([wslab(w_in[l], 1376, 128, 0), wslab(w_x[l], 512, 128, 1024)])
            ps, bp = fm_proj(t2, b2, 0, 128, 0, 128, n)
            if smp:
                psb, bpb = fm_proj(t2, b2, 1024, 128, 0, 128, n)
                rope_apply(KswT[:, tok0:tok0 + n], ps, bp, psb, bpb, 128, n, rope_c64, rope_s64, wbK(B_KswT))
            else:
                copy_op(evac_eng(), KswT[:, tok0:tok0 + n], ps[:, 0:n], [bp], wbK(B_KswT))
            t3, b3 = wload([wslab(w_in[l], 960, 160, 0), wslab(w_x[l], 640, 32, 1280), wslab(w_in[l], 1632, 256, 1536)])
            ps, bp = fm_proj(t3, b3, 0, 160, 0, 128, n)
            ck, bck = TMP()
            ACOPY(ck[:, 0:n], ps[:, 0:n], [bp], [bck])
            rstd_of([(ck[:, 0:n], 128)], n, 128, [bck])
            tt, bt = TMP()
            TT(tt[:, 0:n], ck[:, 0:n], rstd_t[:, 0:n], ALU.mult, [bck, B_rstd], [bt])
            AMUL(ckvn_b[:, 0:n], tt[:, 0:n], V(l, 246), [bt, B_vec], [B_ckvn])
            ps, bp = fm_proj(t3, b3, 0, 160, 128, 32, n, pout=64)
            if smp:
                psb, bpb = fm_proj(t3, b3, 1280, 32, 0, 32, n, pout=64)
                rope_apply(KA[64:96, 0, tok0:tok0 + n], ps, bp, psb, bpb, 32, n, rope_c32, rope_s32, wbK(B_KA), p0=64)
                for h_ in range(1, 4):
                    VCOPY(KA[64:96, h_, tok0:tok0 + n], KA[64:96, 0, tok0:tok0 + n], wbK(B_KA), wbK(B_KA))
            else:
                for h_ in range(4):
                    copy_op(evac_eng(), KA[64:96, h_, tok0:tok0 + n], ps[64:96, 0:n], [bp], wbK(B_KA))
            for m in range(2):
                ps, bp = fm_proj(t3, b3, 1536, 256, m * 128, 128, n)
                if smp:
                    copy_op(evac_eng(), pd_t[:, m, PADP + tok0:PADP + tok0 + n], ps[:, 0:n], [bp], [B_pd])
                else:
                    for sq_ in range(n // SEQ):
                        o_ = sq_ * (SEQ + 2 * PADP) + PADP
                        copy_op(evac_eng(), pd_t[:, m, o_:o_ + SEQ], ps[:, sq_ * SEQ:(sq_ + 1) * SEQ], [bp], [B_pd])
            t4, b4 = load_ukv(l)
            for h_ in range(4):
                ps, bp = PSG()
                mm(ps[0:64, 0:n], t4[:, h_ * 64:(h_ + 1) * 64], ckvn_b[:, 0:n], True, True, [b4, B_ckvn], [bp])
                copy_op(evac_eng(), KA[0:64, h_, tok0:tok0 + n], ps[0:64, 0:n], [bp], wbK(B_KA))
            if smp:
                t5, b5 = wload([wslab(w_in[l], 512, 256, 0), wslab(w_in[l], 1504, 128, 2048)])
            else:
                t5, b5 = wload([wslab(w_in[l], 256, 512, 0)])
                t6, b6 = wload([wslab(w_in[l], 960, 160, 0), wslab(w_in[l], 1376, 256, 1280)])
            for tt_ in range(ncks):
                cidx = c0 + tt_
                hs = lambda k: h_t[:, k, tt_ * 128:(tt_ + 1) * 128]
                if smp:
                    psa, bpa = PSG()
                    for k in range(8):
                        mm(psa[:, 0:256], hs(k), t5[:, k * 256:(k + 1) * 256], k == 0, k == 7, [b5, B_h[k]], [bpa])
                    va_src = psa[:, 0:256]
                    psc, bpc = PSG()
                    for k in range(8):
                        mm(psc[:, 0:128], hs(k), t5[:, 2048 + k * 128:2048 + (k + 1) * 128], k == 0, k == 7, [b5, B_h[k]], [bpc])
                    vc_src = psc[:, 0:128]
                    rv, rc = [bpa], [bpc]
                else:
                    psa, bpa = PSG()
                    for k in range(8):
                        mm(psa[:, 0:512], hs(k), t5[:, k * 512:(k + 1) * 512], k == 0, k == 7, [b5, B_h[k]], [bpa])
                    ACOPY(cst_t[:, 0:512], psa[:, 0:512], [bpa], [B_cst])
                    psb_, bpb_ = PSG()
                    for k in range(8):
                        mm(psb_[:, 0:160], hs(k), t6[:, k * 160:(k + 1) * 160], k == 0, k == 7, [b6, B_h[k]], [bpb_])
                    VCOPY(cst_t[:, 512:672], psb_[:, 0:160], [bpb_], [B_cst])
                    psc, bpc = PSG()
                    for k in range(8):
                        mm(psc[:, 0:256], hs(k), t6[:, 1280 + k * 256:1280 + (k + 1) * 256], k == 0, k == 7, [b6, B_h[k]], [bpc])
                    ACOPY(cst_t[:, 672:928], psc[:, 0:256], [bpc], [B_cst])
                    r0 = st_row0 + tt_ * 128
                    S.dma("sp", st_o[l, r0:r0 + 128, :], cst_t[:, :], reads=[B_cst], final=True)
                    va_src = cst_t[:, 256:512]
                    vc_src = cst_t[:, 800:928]
                    rv, rc = [B_cst], [B_cst]
                VCOPY(vpair_dst(Vna, cidx, 0), vpair_src(va_src, 0), rv, [B_Vna[cidx]])
                VCOPY(vpair_dst(Vna, cidx, 2), vpair_src(va_src, 1), rv, [B_Vna[cidx]])
                ACOPY(Vsw[:, cidx, 0:64], vc_src[:, 0:64], rc, [B_Vsw[cidx]])
                ACOPY(Vsw[:, cidx, 128:192], vc_src[:, 64:128], rc, [B_Vsw[cidx]])
                psv, bpv = PSG()
                mm(psv[:, 0:256], ckvn_b[:, tt_ * 128:(tt_ + 1) * 128], t4[:, 256:512], True, True, [b4, B_ckvn], [bpv])
                VCOPY(vpair_dst(Vml, cidx, 0), vpair_src(psv[:, 0:256], 0), [bpv], [B_Vml[cidx]])
                ACOPY(vpair_dst(Vml, cidx, 2), vpair_src(psv[:, 0:256], 1), [bpv], [B_Vml[cidx]])

        def transpose_to(dst, src_bf, nparts_out, breads, bwrite):
            S.op("pe", lambda e: e.transpose(pst[0:nparts_out, 0:128], src_bf, id_bf[:, :]), reads=breads + [B_const], writes=[B_pst])
            copy_op(evac_eng(), dst, pst[0:nparts_out, 0:128], [B_pst], bwrite)

        def ctx_prep(l):
            def load_tm(src, ncol):
                S.dma("sp", ld_t[:, :, 0:ncol], src.rearrange("(c p) f -> p c f", p=128), writes=[B_ld])
            load_tm(cna_k[l], 256)
            VCOPY(ldb_t[:, :, :], ld_t[:, :, :], [B_ld], [B_ldb])
            for c in range(2):
                for m in range(2):
                    transpose_to(KnaTc[:, m, c * 128:(c + 1) * 128], ldb_t[:, c, m * 128:(m + 1) * 128], 128, [B_ldb], [B_ctx])
            load_tm(cna_v[l], 256)
            for c in range(2):
                for b_ in range(2):
                    VCOPY(vpair_dst(Vnac, c, 2 * b_), vpair_src(ld_t[:, c, :], b_), [B_ld], [B_ctx])
            load_tm(csw_k[l], 128)
            VCOPY(ldb_t[:, :, 0:128], ld_t[:, :, 0:128], [B_ld], [B_ldb])
            for c in range(2):
                transpose_to(KswTc[:, c * 128:(c + 1) * 128], ldb_t[:, c, 0:128], 128, [B_ldb], [B_ctx])
            load_tm(csw_v[l], 128)
            for c in range(2):
                VCOPY(Vswc[:, c, 0:64], ld_t[:, c, 0:64], [B_ld], [B_ctx])
                VCOPY(Vswc[:, c, 128:192], ld_t[:, c, 64:128], [B_ld], [B_ctx])
            S.dma("sp", ld_t[:, :, 64:96], ckrope[l].rearrange("(c p) f -> p c f", p=128), writes=[B_ld])
            VCOPY(ldb_t[:, :, 0:96], ld_t[:, :, 0:96], [B_ld], [B_ldb])
            for c in range(2):
                S.op("pe", lambda e, c=c: e.transpose(pst[0:96, 0:128], ldb_t[:, c, 0:96], id_bf[:, :]), reads=[B_ldb, B_const], writes=[B_pst])
                for h_ in range(4):
                    copy_op(evac_eng(), KAc[64:96, h_, c * 128:(c + 1) * 128], pst[64:96, 0:128], [B_pst], [B_ctx])
            load_tm(cckv[l], 128)
            t4, b4 = load_ukv(l)
            for c in range(2):
                ss, bs = TMP()
                jk, bj = TMP()
                MEMSET(ss[:, 500:501], 0.0, [bs])
                ACT(jk[:, 0:128], ld_t[:, c, 0:128], AF.Square, [B_ld, bs], [bs, bj], accum=ss[:, 500:501])
                ACT(ss[:, 501:502], ss[:, 500:501], AF.Sqrt, [bs, B_vec], [bs], bias=eps_ap, scale=1.0 / 128)
                RECIP(ss[:, 502:503], ss[:, 501:502], [bs], [bs])
                TSMUL(ldb_t[:, c, 0:128], ld_t[:, c, 0:128], ss[:, 502:503], [bs, B_ld], [B_ldb])
                S.op("pe", lambda e, c=c: e.transpose(pst[:, 0:128], ldb_t[:, c, 0:128], id_bf[:, :]), reads=[B_ldb, B_const], writes=[B_pst])
                AMUL(ckvn_b[:, c * 128:(c + 1) * 128], pst[:, 0:128], V(l, 246), [B_pst, B_vec], [B_ckvn])
            for h_ in range(4):
                ps, bp = PSG()
                mm(ps[0:64, 0:256], t4[:, h_ * 64:(h_ + 1) * 64], ckvn_b[:, 0:256], True, True, [b4, B_ckvn], [bp])
                copy_op(evac_eng(), KAc[0:64, h_, :], ps[0:64, 0:256], [bp], [B_ctx])
            for c in range(2):
                psv, bpv = PSG()
                mm(psv[:, 0:256], ckvn_b[:, c * 128:(c + 1) * 128], t4[:, 256:512], True, True, [b4, B_ckvn], [bpv])
                for b_ in range(2):
                    VCOPY(vpair_dst(Vmlc, c, 2 * b_), vpair_src(psv[:, 0:256], b_), [bpv], [B_ctx])

        def phase_B(l, cv, grp, blk, tok0, n):
            smp = grp == "s"
            if smp:
                load_rope(tok0, n)
            t1, b1 = wload([wslab(w_in[l], 0, 256, 0), wslab(w_x[l], 0, 256, 2048)])
            for m in range(2):
                ps, bp = fm_proj(t1, b1, 0, 256, m * 128, 128, n)
                copy_op(evac_eng(), qna[:, m, 0:n], ps[:, 0:n], [bp], [B_qna], scale=ATT_SCALE)
            if smp:
                t2, b2 = wload([wslab(w_x[l], 256, 256, 0)])
            for m in range(2):
                ps, bp = fm_proj(t1, b1, 2048, 256, m * 128, 128, n)
                if smp:
                    psb, bpb = fm_proj(t2, b2, 0, 256, m * 128, 128, n)
                    rope_apply(qsw[:, m, 0:n], ps, bp, psb, bpb, 128, n, rope_c64, rope_s64, [B_qsw])
                else:
                    copy_op(evac_eng(), qsw[:, m, 0:n], ps[:, 0:n], [bp], [B_qsw])
            t3, b3 = wload([wslab(w_in[l], 768, 192, 0)])
            cq = []
            for (m0, mw) in ((0, 128), (128, 64)):
                ps, bp = fm_proj(t3, b3, 0, 192, m0, mw, n)
                ck, bck = TMP()
                copy_op(evac_eng(), ck[0:mw, 0:n], ps[0:mw, 0:n], [bp], [bck])
                cq.append((ck, bck, mw))
            rstd_of([(ck[0:mw, 0:n], mw) for (ck, bck, mw) in cq], n, 192, [c_[1] for c_ in cq])
            for j, (ck, bck, mw) in enumerate(cq):
                tt, bt = TMP()
                TT(tt[0:mw, 0:n], ck[0:mw, 0:n], rstd_t[0:mw, 0:n], ALU.mult, [bck, B_rstd], [bt])
                AMUL(cqn[0:mw, j, 0:n], tt[0:mw, 0:n], vec_t[0:mw, l, 244 + j:245 + j], [bt, B_vec], [B_cqn])
            i = _wi[0] % NSLOT
            _wi[0] += 1
            t4, b4 = ws[i], B_ws[i]
            first = True
            for k, (r0, nr) in enumerate(((0, 128), (128, 64))):
                uq3 = w_uq[l, r0:r0 + nr, :].rearrange("k (h c) -> k h c", h=4)
                S.dma("pool", t4[0:nr, k * 512:k * 512 + 256].rearrange("p (h c) -> p h c", h=4), uq3[:, :, 0:64], writes=[b4], partial=not first)
                first = False
                S.dma("pool", t4[0:nr, k * 512 + 256:k * 512 + 384].rearrange("p (h c) -> p h c", h=4), uq3[:, :, 64:96], writes=[b4], partial=True)
                S.dma("pool", t4[0:nr, k * 512 + 384:k * 512 + 512], w_uqp[l, r0:r0 + nr, :], writes=[b4], partial=True)
            rhs_cq = lambda k, kr: cqn[0:kr, k, 0:n]
            for hh in range(4):
                ps, bp = fm_proj(t4, b4, 0, 512, hh * 64, 64, n, nk=2, krows=[128, 64], rhs=rhs_cq, rreads=[B_cqn])
                copy_op(evac_eng(), QA[0:64, hh, 0:n], ps[0:64, 0:n], [bp], [B_QA])
            for hh in range(4):
                ps, bp = fm_proj(t4, b4, 0, 512, 256 + 32 * hh, 32, n, nk=2, krows=[128, 64], rhs=rhs_cq, rreads=[B_cqn], pout=64)
                if smp:
                    psb, bpb = fm_proj(t4, b4, 0, 512, 384 + 32 * hh, 32, n, nk=2, krows=[128, 64], rhs=rhs_cq, rreads=[B_cqn], pout=64)
                    rope_apply(QA[64:96, hh, 0:n], ps, bp, psb, bpb, 32, n, rope_c32, rope_s32, [B_QA], p0=64)
                else:
                    copy_op(evac_eng(), QA[64:96, hh, 0:n], ps[64:96, 0:n], [bp], [B_QA])
            tp, bpw = wload([(0, poolbd[l].rearrange("(c p) f -> p c f", p=128), 128, 2, 128)])
            if smp:
                segs = [(PADP + tok0, 0, n, tok0 == 0, tok0 + n == LS)]
            else:
                segs = [(s_ * (SEQ + 2 * PADP) + PADP, s_ * SEQ, SEQ, True, True) for s_ in range(n // SEQ)]
            ext = [7, 6, 4, 0]
            shf = [None, 1, 2, 4]
            for c in range(2):
                for (po_, o0, ln, at_start, at_end) in segs:
                    for half in range(2):
                        g = 2 * c + half
                        p0, p1 = 64 * half, 64 * half + 64
                        src = lambda a, b_: pd_t[p0:p1, c, po_ + a:po_ + b_]
                        cur = None
                        for lev in range(g + 1):
                            a, b_ = -ext[lev], ln + ext[lev]
                            dst_t, dst_b = pw_t[lev % 2], B_pw[lev % 2]
                            dst = dst_t[p0:p1, PADP + a:PADP + b_]
                            if lev == 0:
                                TT(dst, src(a - 1, b_ - 1), src(a, b_), ALU.add, [B_pd], [dst_b])
                            else:
                                s_ = shf[lev]
                                pv_t, pv_b = pw_t[(lev - 1) % 2], B_pw[(lev - 1) % 2]
                                TT(dst, pv_t[p0:p1, PADP + a - s_:PADP + b_ - s_], pv_t[p0:p1, PADP + a + s_:PADP + b_ + s_], ALU.add,
                                   [pv_b], [dst_b])
                            cur = (dst_t, dst_b)
                        wt, wb_ = cur
                        tt, bt = TMP()
                        STT(tt[p0:p1, 0:ln], wt[p0:p1, PADP:PADP + ln], vec_t[p0:p1, 0, 253 + c:254 + c], src(0, ln), ALU.mult, ALU.subtract,
                            [wb_, B_pd, B_vec], [bt])
                        if at_start:
                            t2_, b2_ = TMP()
                            TT(t2_[p0:p1, 0:8], wt[p0:p1, PADP:PADP + 8], vec_t[p0:p1, 0, 256 + 16 * c:256 + 16 * c + 8], ALU.mult, [wb_, B_vec], [b2_])
                            TT(tt[p0:p1, 0:8], t2_[p0:p1, 0:8], src(0, 8), ALU.subtract, [b2_, B_pd, bt], [bt])
                        if at_end:
                            t2_, b2_ = TMP()
                            TT(t2_[p0:p1, 0:8], wt[p0:p1, PADP + ln - 8:PADP + ln], vec_t[p0:p1, 0, 256 + 16 * c + 8:256 + 16 * c + 16], ALU.mult,
                               [wb_, B_vec], [b2_])
                            TT(tt[p0:p1, ln - 8:ln], t2_[p0:p1, 0:8], src(ln - 8, ln), ALU.subtract, [b2_, B_pd, bt], [bt])
                        ACOPY(dl_t[p0:p1, o0:o0 + ln], tt[p0:p1, 0:ln], [bt], [B_dl])
                ps, bp = PSG()
                mm(ps[:, 0:n], tp[:, c * 128:(c + 1) * 128], dl_t[:, 0:n], True, True, [bpw, B_dl], [bp])
                AMUL(br_t[:, 6 + c, 0:n], ps[:, 0:n], V(l, 247 + c), [bp, B_vec], [B_br[6 + c]])
            if smp:
                qranges = [(0, n)]
            else:
                qranges = [(s_ * SEQ, SEQ) for s_ in range(n // SEQ)]
            for (q0, nq) in qranges:
                if smp:
                    na_list = [(c, NA_TILE0[blk] + j) for j, c in enumerate(NA_CHUNKS[blk])]
                    sw_list = swa_chunks(blk)
                    ml_list = list(range(16))
                else:
                    cc = (tok0 + q0) // 128
                    na_list = [(cc, None), (cc + 1, None)]
                    sw_list = [cc, cc + 1]
                    ml_list = [cc, cc + 1]
                for hh in range(4):
                    m, par = hh // 2, hh % 2
                    pb = 64 * par
                    vo = m * 192 + par * 64
                    chunks = []
                    if smp:
                        for c in range(2):
                            chunks.append(dict(k=KnaTc[pb:pb + 64, m, c * 128:(c + 1) * 128], v=Vnac[:, c, vo:vo + 128], reads=[B_ctx]))
                    for (c, tile_i) in na_list:
                        d = dict(k=KnaT[pb:pb + 64, m, c * 128:(c + 1) * 128], v=Vna[:, c, vo:vo + 128], reads=[B_KnaT[c], B_Vna[c]])
                        if tile_i is not None:
                            d["bias_src"] = nabias[l, hh, tile_i]
                            d["qr"] = na_qrange(blk, c)
                        chunks.append(d)
                    if smp:
                        chunks.append(chunks.pop(1))
                    attend(qna[pb:pb + 64, m, q0:q0 + nq], None, [B_qna], chunks, nq, br_t[pb:pb + 64, m, q0:q0 + nq], pb, pb,
                           1.0, None, [B_br[m]])
                    chunks = []
                    if smp:
                        for c in range(2):
                            chunks.append(dict(k=KAc[:, hh, c * 128:(c + 1) * 128], v=Vmlc[:, c, vo:vo + 128], reads=[B_ctx]))
                    for c in ml_list:
                        chunks.append(dict(k=KA[:, hh, c * 128:(c + 1) * 128], v=Vml[:, c, vo:vo + 128], reads=[B_KA[c], B_Vml[c]]))
                    attend(QA[:, hh, q0:q0 + nq], None, [B_QA], chunks, nq,
                           br_t[pb:pb + 64, 2 + m, q0:q0 + nq], pb, pb, MLA_SCALE, None, [B_br[2 + m]])
                    kv = hh // 2
                    qc_, qp = hh % 2, 64 * (hh // 2)
                    chunks = []
                    if smp:
                        for c in range(2):
                            chunks.append(dict(k=KswTc[qp:qp + 64, c * 128:(c + 1) * 128], v=Vswc[:, c, kv * 64:kv * 64 + 128], reads=[B_ctx]))
                    for c in sw_list:
                        d = dict(k=KswT[qp:qp + 64, c * 128:(c + 1) * 128], v=Vsw[:, c, kv * 64:kv * 64 + 128], reads=[B_KswT[c], B_Vsw[c]])
                        if smp:
                            d["bias_src"] = swmask[c - (4 * blk - 1)]
                            d["qr"] = (max(0, 128 * c - 128 - 512 * blk), min(512, 128 * c + 256 - 512 * blk))
                        chunks.append(d)
                    if smp:
                        chunks.append(chunks.pop(1))
                    attend(qsw[qp:qp + 64, qc_, q0:q0 + nq], None, [B_qsw], chunks, nq, br_t[pb:pb + 64, 4 + m, q0:q0 + nq], pb, 64 * kv,
                           ATT_SCALE, esink[:, l, hh:hh + 1], [B_br[4 + m]])
            for kb in range(4):
                for q4 in range(4):
                    tg, bg = wload([wslab(w_gate[l], 1024 * kb + 256 * q4, 256, 0),
                                    (2048, w_branch[l, 256 * kb:256 * kb + 256, 256 * q4:256 * q4 + 256].rearrange("(k p) c -> p k c", p=128),
                                     128, 2, 256)])
                    for m2 in range(2):
                        m = q4 * 2 + m2
                        psg_, bpg = fm_proj(tg, bg, 0, 256, m2 * 128, 128, n)
                        gt_, bgt = TMP()
                        ACT(gt_[:, 0:n], psg_[:, 0:n], AF.Sigmoid, [bpg, B_vec], [bgt], bias=V(l, 80 + 8 * kb + m), scale=1.0)
                        psp, bpp = PSG()
                        for k in range(2):
                            mm(psp[:, 0:n], tg[:, 2048 + k * 256 + m2 * 128:2048 + k * 256 + (m2 + 1) * 128], br_t[:, 2 * kb + k, 0:n], k == 0, k == 1,
                               [bg, B_br[2 * kb + k]], [bpp])
                        if kb == 0:
                            TT(mg_f[:, m, 0:n], psp[:, 0:n], gt_[:, 0:n], ALU.mult, [bpp, bgt], [B_wk[m]])
                        else:
                            TT(gt_[:, 0:n], psp[:, 0:n], gt_[:, 0:n], ALU.mult, [bpp, bgt], [bgt])
                            TT(mg_f[:, m, 0:n], mg_f[:, m, 0:n], gt_[:, 0:n], ALU.add, [bgt, B_wk[m]], [B_wk[m]])
            for m in range(8):
                copy_op(evac_eng(), h_t[:, m, 0:n], mg_f[:, m, 0:n], [B_wk[m]], [B_h[m]])
            for half in range(2):
                to, bo = wload([wslab(w_out[l], 512 * half, 512)])
                for m4 in range(4):
                    m = half * 4 + m4
                    ps, bp = fm_proj(to, bo, 0, 512, m4 * 128, 128, n)
                    copy_op(evac_eng(), mg_f[:, m, 0:n], ps[:, 0:n], [bp], [B_wk[m]])
            post_norm_residual(l, cv, 2, mg_f, B_wk, n)

        def phase_C(l, cv, n, segs):
            norm_to_h(l, cv, 3, n)
            _wide[0] = True
            prev_ = [None]

            def flush_prev():
                accs_, mw_, j_ = prev_[0]
                (aa, ba), (gg, bgg) = accs_
                ACT(gg[0:mw_, 0:n], gg[0:mw_, 0:n], AF.Silu, [bgg], [bgg])
                TT(act_t[0:mw_, j_, 0:n], aa[0:mw_, 0:n], gg[0:mw_, 0:n], ALU.mult, [ba, bgg], [B_act[j_]])
                prev_[0] = None
            xfer(B_Vna + B_Vml, B_act)
            for j2 in range(11):
                a0 = 256 * j2
                na = min(256, D_FF - a0)
                tu, bu = wload([wslab(w_up[l], a0, na, 0), wslab(w_up[l], D_FF + a0, na, 2048)])
                for jj in range((na + 127) // 128):
                    j = 2 * j2 + jj
                    mw = min(128, na - 128 * jj)
                    pp = [fm_proj(tu, bu, 2048 * part, na, 128 * jj, mw, n) for part in range(2)]
                    accs = [TMP() for part in range(2)]
                    cws = [(lambda tap, cidx=j + 22 * part: vec_t[0:mw, l, 112 + 44 * tap + cidx:113 + 44 * tap + cidx]) for part in range(2)]
                    ln = segs[0][1]
                    nsg = len(segs)

                    def v3(ap_, a_, b_):
                        return ap_[0:mw, 0:n].rearrange("p (s t) -> p s t", s=nsg)[:, :, a_:b_]
                    for part in range(2):
                        (ps, bp), (acc, bacc) = pp[part], accs[part]
                        AMUL(acc[0:mw, 0:n], ps[0:mw, 0:n], cws[part](1), [bp, B_vec], [bacc])
                    for part in range(2):
                        (ps, bp), (acc, bacc) = pp[part], accs[part]
                        STT(v3(acc, 1, ln), v3(ps, 0, ln - 1), cws[part](0), v3(acc, 1, ln), ALU.mult, ALU.add, [bp, B_vec, bacc], [bacc])
                    for part in range(2):
                        (ps, bp), (acc, bacc) = pp[part], accs[part]
                        STT(v3(acc, 0, ln - 1), v3(ps, 1, ln), cws[part](2), v3(acc, 0, ln - 1), ALU.mult, ALU.add, [bp, B_vec, bacc], [bacc])
                    if prev_[0] is not None:
                        flush_prev()
                    prev_[0] = (accs, mw, j)
            if prev_[0] is not None:
                flush_prev()
            for m in range(8):
                td, bd = wload([(0, w_down[l, 0:2688, 128 * m:128 * m + 128].rearrange("(k p) c -> p k c", p=128), 128, 21, 128),
                                (21 * 128, w_down[l, 2688:2752, 128 * m:128 * m + 128], 64, 0, 128)])
                ps, bp = PSG()
                for k in range(22):
                    kr = 128 if k < 21 else 64
                    mm(ps[:, 0:n], td[0:kr, k * 128:(k + 1) * 128], act_t[0:kr, k, 0:n], k == 0, k == 21, [bd, B_act[k]], [bp])
                if m == 0:
                    xfer([B_pd], B_ov)
                copy_op(evac_eng(), o_v[:, m, 0:n], ps[:, 0:n], [bp], [B_ov[m]])
            post_norm_residual(l, cv, 5, o_v, B_ov, n)
            _wide[0] = False
            xfer(B_ov, [B_pd])
            for (pa_, pb2_) in ((0, 16), (272, 304), (560, 576), (2064, 2080)):
                MEMSET(pd_t[:, :, pa_:pb2_], 0.0, [B_pd])

        B_XM, B_X1, B_ys = B("XM"), B("X1"), B("ys")

        XB = [(x_t, B_x), (mg_f, B_wk)]

        def load_x(src, c0, n, rb, xb=0):
            xt_, bx_ = XB[xb]
            S.dma("sp", xt_[:, :, 0:n], chunks_of(src)[:, :, c0:c0 + n], reads=rb, writes=bx_)

        def store_x(dst, c0, lo, hi, wb_, final=False, xb=0):
            xt_, bx_ = XB[xb]
            S.dma("sp", chunks_of(dst)[:, :, lo:hi], xt_[:, :, lo - c0:hi - c0], reads=bx_, writes=wb_, sem_buf=bx_[0], final=final)

        def use_x(xb):
            cur_x[0], cur_x[1] = XB[xb]

        mods_done = set()

        def need_mods(l):
            if l not in mods_done:
                mods_done.add(l)
                compute_mods(l)

        if do_sample:
            for l in range(DEPTH):
                need_mods(l)
                src = xsT if l == 0 else X1
                src_b = [] if l == 0 else [B_X1]
                begin_group()
                load_x(src, 0, NB, src_b, xb=0)
                for blk in range(4):
                    if blk + 1 < 4:
                        load_x(src, (blk + 1) * NB, NB, src_b, xb=(blk + 1) % 2)
                    use_x(blk % 2)
                    phase_A(l, 1, "s", blk * NB, NB)
                ctx_prep(l)
                use_x(0)
                for blk in range(4):
                    load_x(src, blk * NB, NB, src_b)
                    norm_to_h(l, 1, 0, NB)
                    phase_B(l, 1, "s", blk, blk * NB, NB)
                    store_x(XM, blk * NB, blk * NB, blk * NB + NB, [B_XM])
                dst = X1 if l == 0 else ysT
                dst_b = [B_X1] if l == 0 else [B_ys]
                load_x(XM, SWA_WIN[0][0], NB, [B_XM], xb=0)
                for wi, (w0, lo, hi) in enumerate(SWA_WIN):
                    if wi + 1 < len(SWA_WIN):
                        load_x(XM, SWA_WIN[wi + 1][0], NB, [B_XM], xb=(wi + 1) % 2)
                    use_x(wi % 2)
                    phase_C(l, 1, NB, [(0, NB)])
                    store_x(dst, w0, lo, hi, dst_b, final=(l == DEPTH - 1), xb=wi % 2)
                use_x(0)
        if do_prompt:
            use_x(0)
            for pb_ in range(2):
                load_x(xpT, pb_ * NB, NB, [])
                for l in range(dbg_layers):
                    need_mods(l)
                    begin_group(4)
                    phase_A(l, 0, "p", 0, NB, st_row0=pb_ * NB)
                    if not dbg_skipB:
                        phase_B(l, 0, "p", 0, 0, NB)
                    if not dbg_skipC:
                        phase_C(l, 0, NB, [(0, SEQ), (SEQ, SEQ)])
                store_x(ypT, pb_ * NB, pb_ * NB, pb_ * NB + NB, [], final=True)
        S.emit()
    return nc


def _rope_partner(d):
    h = d // 2
    hp = h // 2
    idx = np.arange(d)
    first = (idx % h) < hp
    return np.where(first, idx + hp, idx - hp), first


def _rope_tables(d, reps):
    h = d // 2
    hp = h // 2
    t = np.arange(LS)
    pos = np.stack([t // GRID_W, t % GRID_W], 0).astype(np.float32)
    idx = np.arange(d)
    sec = idx // h
    i = (idx % h) % hp
    inv = (10000.0 ** (-(np.arange(hp, dtype=np.float32)) / hp)).astype(np.float32)
    ang = pos[sec] * inv[i][:, None]
    _, first = _rope_partner(d)
    cos = np.cos(ang).astype(np.float32)
    sin = np.sin(ang).astype(np.float32) * np.where(first, -1.0, 1.0).astype(np.float32)[:, None]
    return np.tile(cos, (reps, 1)), np.tile(sin, (reps, 1))


def _chunkcols(v):
    return np.ascontiguousarray(v.reshape(-1, 128).T)


def _na_bias(rpb):
    k = np.arange(LS)
    q = np.arange(LS)
    kr, kc = k // 64, k % 64
    r, c = q // 64, q % 64
    rs = np.clip(r - 4, 0, 24)
    cs = np.clip(c - 8, 0, 48)
    valid = (kr[:, None] >= rs[None, :]) & (kr[:, None] < rs[None, :] + 8) & \
            (kc[:, None] >= cs[None, :]) & (kc[:, None] < cs[None, :] + 16)
    dr = np.clip(kr[:, None] - r[None, :] + 7, 0, 14)
    dc = np.clip(kc[:, None] - c[None, :] + 15, 0, 30)
    out = np.empty((DEPTH, 4, 28, 128, 512), np.float32)
    for blk in range(4):
        for j, ch in enumerate(NA_CHUNKS[blk]):
            ks = slice(ch * 128, ch * 128 + 128)
            qs = slice(blk * 512, blk * 512 + 512)
            g = rpb[:, :, dr[ks, qs], dc[ks, qs]]
            out[:, :, NA_TILE0[blk] + j] = np.where(valid[ks, qs][None, None], g, np.float32(NEG))
    return out


def _sw_mask():
    out = np.empty((6, 128, 512), np.float32)
    kl = np.arange(128)[:, None]
    ql = np.arange(512)[None, :]
    for i in range(6):
        delta = (i - 1) * 128
        out[i] = np.where(np.abs(ql - (kl + delta)) <= 128, 0.0, NEG)
    return out


def _prep_shared(inp):
    f = lambda a: np.ascontiguousarray(np.asarray(a, dtype=np.float32))
    sh = {}
    for k in ("w_mod", "w_in", "w_gate", "w_out", "mla_w_uq", "mla_w_ukv"):
        sh[k] = f(inp[k])
    sh["w_branch"] = f(inp["w_branch"]).reshape(DEPTH, D, D)
    sh["w_up"] = f(inp["ffn_w_up"])
    sh["w_down"] = f(inp["ffn_w_down"])
    w_in = sh["w_in"]
    p64, _ = _rope_partner(64)
    p32, _ = _rope_partner(32)
    qc = w_in[:, :, 1120:1376].reshape(DEPTH, D, 4, 64)
    order = [0, 2, 1, 3]
    qc_r = qc[:, :, order, :]
    qc_rp = qc_r[:, :, :, p64]
    kc = w_in[:, :, 1376:1504].reshape(DEPTH, D, 2, 64)
    kc_p = kc[:, :, :, p64]
    krp = w_in[:, :, 1088:1120][:, :, p32]
    sh["w_x"] = np.ascontiguousarray(np.concatenate(
        [qc_r.reshape(DEPTH, D, 256), qc_rp.reshape(DEPTH, D, 256), kc_p.reshape(DEPTH, D, 128), krp], -1))
    uq = sh["mla_w_uq"].reshape(DEPTH, 192, 4, 96)
    sh["w_uqp"] = np.ascontiguousarray(uq[:, :, :, 64:96][:, :, :, p32].reshape(DEPTH, 192, 128))
    pw = f(inp["pool_w"])
    bd = np.zeros((DEPTH, 2, 128, 128), np.float32)
    for c in range(2):
        for hf in range(2):
            bd[:, c, 64 * hf:64 * hf + 64, 64 * hf:64 * hf + 64] = pw[:, 2 * c + hf]
    sh["poolbd"] = bd.reshape(DEPTH, 256, 128)
    vecs = np.zeros((DEPTH, 128, NV), np.float32)
    for l in range(DEPTH):
        v = vecs[l]
        v[:, 0:8] = _chunkcols(f(inp["g_attn_pre"])[l])
        v[:, 8:16] = _chunkcols(f(inp["g_attn_post"])[l])
        v[:, 16:24] = _chunkcols(f(inp["g_ffn_pre"])[l])
        v[:, 24:32] = _chunkcols(f(inp["g_ffn_post"])[l])
        v[:, 32:80] = _chunkcols(f(inp["b_mod"])[l])
        v[:, 80:112] = _chunkcols(f(inp["b_gate"])[l])
        cw = f(inp["ffn_conv"])[l]
        for tap in range(3):
            for part in range(2):
                seg = np.zeros(22 * 128, np.float32)
                seg[:D_FF] = cw[tap, part * D_FF:(part + 1) * D_FF]
                v[:, 112 + 44 * tap + 22 * part:112 + 44 * tap + 22 * part + 22] = _chunkcols(seg)
        qn = np.zeros(256, np.float32)
        qn[:192] = f(inp["mla_q_norm"])[l]
        v[:, 244:246] = _chunkcols(qn)
        v[:, 246] = f(inp["mla_kv_norm"])[l]
        v[:, 247:249] = _chunkcols(f(inp["pool_scale"])[l])
        v[:, 249:253] = f(inp["swa_sink"])[l][None, :]
        wins = np.array([2, 4, 8, 16], np.float32)
        for c in range(2):
            wp = np.repeat(wins[2 * c:2 * c + 2], 64)
            v[:, 253 + c] = 1.0 / wp
            for e_ in range(8):
                t = e_
                v[:, 256 + 16 * c + e_] = 1.0 / (t + wp / 2 - np.maximum(t - wp / 2, 0))
                d_end = 8 - e_
                v[:, 256 + 16 * c + 8 + e_] = 1.0 / (np.minimum(wp / 2, d_end) + wp / 2)
        v[:, 255] = EPS
    sh["vecs"] = vecs
    c64, s64 = _rope_tables(64, 2)
    c32, s32 = _rope_tables(32, 1)
    sh["cos64"], sh["sin64"], sh["cos32"], sh["sin32"] = c64, s64, c32, s32
    sh["nabias"] = _na_bias(f(inp["na_rpb"]))
    sh["swmask"] = _sw_mask()
    return sh


_NC_CACHE = {}


def kernel(**inputs):
    f = lambda a: np.ascontiguousarray(np.asarray(a, dtype=np.float32))
    sh = _prep_shared(inputs)
    x_prompt = f(inputs["x_prompt"])
    x_sample = f(inputs["x_sample"])
    c = f(inputs["c"])
    c_ctx = f(inputs["c_ctx"])
    if "nc" not in _NC_CACHE:
        _NC_CACHE["nc"] = build_nc()
    nc = _NC_CACHE["nc"]
    in_maps = []
    for j in range(8):
        b = j // 4
        m = {}
        m["xpT"] = np.ascontiguousarray(x_prompt[4 * j:4 * j + 4].reshape(1024, D).T)
        m["xsT"] = np.ascontiguousarray(x_sample[b].T)
        cv = np.stack([_chunkcols(c_ctx), _chunkcols(c[b])], -1)
        m["cvec"] = np.ascontiguousarray(cv.reshape(128, 16))
        m["vecs"] = sh["vecs"]
        for k in ("w_mod", "w_in", "w_x", "w_gate", "w_branch", "w_out", "w_up", "w_down", "w_uqp", "poolbd",
                  "cos64", "sin64", "cos32", "sin32", "nabias", "swmask"):
            m[k] = sh[k]
        m["w_uq"] = sh["mla_w_uq"]
        m["w_ukv"] = sh["mla_w_ukv"]
        m["cna_k"] = np.ascontiguousarray(f(inputs["cache_na_k"])[b].reshape(DEPTH, 256, 256))
        m["cna_v"] = np.ascontiguousarray(f(inputs["cache_na_v"])[b].reshape(DEPTH, 256, 256))
        m["cckv"] = np.ascontiguousarray(f(inputs["cache_mla_ckv"])[b])
        m["ckrope"] = np.ascontiguousarray(f(inputs["cache_mla_krope"])[b])
        m["csw_k"] = np.ascontiguousarray(f(inputs["cache_swa_k"])[b].reshape(DEPTH, 256, 128))
        m["csw_v"] = np.ascontiguousarray(f(inputs["cache_swa_v"])[b].reshape(DEPTH, 256, 128))
        in_maps.append(m)
    res = run_bass_kernel_spmd(nc, in_maps, core_ids=list(range(8)))
    R = res.results
    y_prompt = np.concatenate([R[j]["ypT"].T.reshape(4, SEQ, D) for j in range(8)], 0).astype(np.float32)
    y_sample = np.stack([R[0]["ysT"].T, R[4]["ysT"].T], 0).astype(np.float32)
    st = np.concatenate([R[j]["st"].reshape(DEPTH, 4, SEQ, ST_W).transpose(1, 0, 2, 3) for j in range(8)], 0)
    new_na_k = np.ascontiguousarray(st[..., 0:256]).reshape(32, DEPTH, SEQ, 4, 64)
    new_na_v = np.ascontiguousarray(st[..., 256:512]).reshape(32, DEPTH, SEQ, 4, 64)
    new_ckv = np.ascontiguousarray(st[..., 512:640])
    new_kr = np.ascontiguousarray(st[..., 640:672])
    new_sk = np.ascontiguousarray(st[..., 672:800]).reshape(32, DEPTH, SEQ, 2, 64)
    new_sv = np.ascontiguousarray(st[..., 800:928]).reshape(32, DEPTH, SEQ, 2, 64)
    return (np.ascontiguousarray(y_prompt), np.ascontiguousarray(y_sample), new_na_k, new_na_v, new_ckv, new_kr, new_sk, new_sv)
```
